# Optimizing a Trainium2 kernel written in Bass

```python
import math
import jax, jax.numpy as jnp
from jax import lax
import numpy as np

D_MODEL = 2048
BATCH = 4
SEQ = 8192
DEPTH = 4

CHUNK = 64
N_META = 16
Q_BLOCK = 128
NORM_EPS = 1e-6
NEG_INF = -1e30

CONV_WIDTH = 1024
CONV_KERNEL = 31

DIFF_HEADS = 8
DIFF_HEAD_DIM = 64
DIFF_V_DIM = 2 * DIFF_HEAD_DIM
REL_BUCKETS = 32
REL_MAX_DIST = 128

MLA_HEADS = 8
MLA_Q_RANK = 512
MLA_KV_RANK = 256
MLA_NOPE_DIM = 128
MLA_ROPE_DIM = 64
MLA_V_DIM = 128
ROPE_THETA = 10000.0

D_FF = -(-8 * D_MODEL // (3 * 256)) * 256

N_BRANCH = 3
OFF_A = 0
OFF_BQ = OFF_A + 2 * CONV_WIDTH
DIFF_QK = DIFF_HEADS * 2 * DIFF_HEAD_DIM
OFF_BK = OFF_BQ + DIFF_QK
OFF_BV = OFF_BK + DIFF_QK
OFF_CQ = OFF_BV + DIFF_HEADS * DIFF_V_DIM
OFF_CKV = OFF_CQ + MLA_Q_RANK
OFF_CR = OFF_CKV + MLA_KV_RANK
OFF_G = OFF_CR + MLA_ROPE_DIM
IN_COLS = OFF_G + N_BRANCH * D_MODEL

kernel_name = "hybrid_gated_conv_diffattn_mla_block"


def rmsnorm(x, g):
    xf = x.astype(jnp.float32)
    y = xf * lax.rsqrt(jnp.mean(xf * xf, axis=-1, keepdims=True) + NORM_EPS)
    return (y * g.astype(jnp.float32)).astype(x.dtype)


def layernorm(x, g, b):
    xf = x.astype(jnp.float32)
    mu = jnp.mean(xf, axis=-1, keepdims=True)
    var = jnp.mean(jnp.square(xf - mu), axis=-1, keepdims=True)
    y = (xf - mu) * lax.rsqrt(var + NORM_EPS)
    return (y * g.astype(jnp.float32) + b.astype(jnp.float32)).astype(x.dtype)


def heads(t, n, d):
    b, l, _ = t.shape
    return t.reshape(b, l, n, d).transpose(0, 2, 1, 3)


def merge_heads(t):
    b, h, l, d = t.shape
    return t.transpose(0, 2, 1, 3).reshape(b, l, h * d)


def chunk_id(pos):
    return jnp.where(pos < N_META, 0, (pos - N_META) // CHUNK + 1)


def t5_bucket(rel):
    nb = REL_BUCKETS // 2
    ret = jnp.where(rel > 0, nb, 0)
    n = jnp.abs(rel)
    max_exact = nb // 2
    nf = jnp.maximum(n, 1).astype(jnp.float32)
    large = max_exact + (jnp.log(nf / max_exact) / math.log(REL_MAX_DIST / max_exact)
                         * (nb - max_exact)).astype(jnp.int32)
    large = jnp.minimum(large, nb - 1)
    return ret + jnp.where(n < max_exact, n, large)


def rope(x, pos):
    half = MLA_ROPE_DIM // 2
    inv = ROPE_THETA ** (-jnp.arange(half, dtype=jnp.float32) / half)
    ang = pos.astype(jnp.float32)[:, None] * inv[None, :]
    cos, sin = jnp.cos(ang), jnp.sin(ang)
    xf = x.astype(jnp.float32)
    x1, x2 = xf[..., :half], xf[..., half:]
    return jnp.concatenate([x1 * cos - x2 * sin, x2 * cos + x1 * sin], axis=-1).astype(x.dtype)


def sweep_queries(fn, qs, pos):
    meta_out = fn(tuple(q[:, :, :N_META] for q in qs), pos[:N_META])
    n_real = pos.shape[0] - N_META
    nblk = n_real // Q_BLOCK

    def split(q):
        b, h, _, d = q.shape
        return jnp.moveaxis(q[:, :, N_META:].reshape(b, h, nblk, Q_BLOCK, d), 2, 0)

    blocks = tuple(split(q) for q in qs)
    pos_blk = pos[N_META:].reshape(nblk, Q_BLOCK)
    out = lax.map(lambda a: fn(a[0], a[1]), (blocks, pos_blk))
    nb, b, h, qb, dv = out.shape
    out = jnp.moveaxis(out, 0, 2).reshape(b, h, nb * qb, dv)
    return jnp.concatenate([meta_out, out], axis=2)


def causal_depthwise_conv(u, w, bias):
    out = lax.conv_general_dilated(
        u, w[:, None, :].astype(u.dtype), window_strides=(1,),
        padding=((CONV_KERNEL - 1, 0),),
        dimension_numbers=("NWC", "WIO", "NWC"),
        feature_group_count=u.shape[-1])
    return out + bias


def conv_branch(proj, conv_w, conv_b, ln_g, ln_b):
    val = proj[..., OFF_A:OFF_A + CONV_WIDTH]
    gate = proj[..., OFF_A + CONV_WIDTH:OFF_BQ]
    a = val * jax.nn.sigmoid(gate)
    a = causal_depthwise_conv(a, conv_w, conv_b)
    return jax.nn.silu(layernorm(a, ln_g, ln_b))


def diff_branch(proj, pos, rel_table, lq1, lk1, lq2, lk2, subln_g, lam_init):
    q = heads(proj[..., OFF_BQ:OFF_BK], 2 * DIFF_HEADS, DIFF_HEAD_DIM)
    k = heads(proj[..., OFF_BK:OFF_BV], 2 * DIFF_HEADS, DIFF_HEAD_DIM)
    v = heads(proj[..., OFF_BV:OFF_CQ], DIFF_HEADS, DIFF_V_DIM)
    q1, q2 = q[:, 0::2], q[:, 1::2]
    k1, k2 = k[:, 0::2], k[:, 1::2]
    f32 = jnp.float32
    lam = (jnp.exp(jnp.sum(lq1.astype(f32) * lk1.astype(f32)))
           - jnp.exp(jnp.sum(lq2.astype(f32) * lk2.astype(f32))) + lam_init)
    scale = DIFF_HEAD_DIM ** -0.5
    k_chunk = chunk_id(pos)

    def attend(qb, q_pos):
        qa, qc = qb
        vis = k_chunk[None, :] <= chunk_id(q_pos)[:, None]
        bias = jnp.transpose(rel_table[t5_bucket(pos[None, :] - q_pos[:, None])],
                             (2, 0, 1)).astype(f32)

        def probs(qq, kk):
            s = jnp.einsum("bhqd,bhkd->bhqk", qq, kk).astype(f32) * scale + bias
            return jax.nn.softmax(jnp.where(vis, s, NEG_INF), axis=-1)

        p = probs(qa, k1) - lam * probs(qc, k2)
        return jnp.einsum("bhqk,bhkd->bhqd", p.astype(v.dtype), v)

    o = sweep_queries(attend, (q1, q2), pos)
    o = rmsnorm(o, subln_g) * (1.0 - lam_init)
    return merge_heads(o)


def mla_branch(proj, pos, q_norm, w_uq, kv_norm, w_ukv):
    cq = rmsnorm(proj[..., OFF_CQ:OFF_CKV], q_norm)
    q = heads(cq @ w_uq, MLA_HEADS, MLA_NOPE_DIM + MLA_ROPE_DIM)
    q_nope = q[..., :MLA_NOPE_DIM]
    q_rope = rope(q[..., MLA_NOPE_DIM:], pos)
    ckv = rmsnorm(proj[..., OFF_CKV:OFF_CR], kv_norm)
    kv = heads(ckv @ w_ukv, MLA_HEADS, MLA_NOPE_DIM + MLA_V_DIM)
    k_nope = kv[..., :MLA_NOPE_DIM]
    v = kv[..., MLA_NOPE_DIM:]
    k_rope = rope(proj[..., OFF_CR:OFF_G], pos)
    scale = (MLA_NOPE_DIM + MLA_ROPE_DIM) ** -0.5
    k_chunk = chunk_id(pos)

    def attend(qb, q_pos):
        qn, qr = qb
        vis = k_chunk[None, :] <= chunk_id(q_pos)[:, None]
        s = (jnp.einsum("bhqd,bhkd->bhqk", qn, k_nope)
             + jnp.einsum("bhqr,bkr->bhqk", qr, k_rope)).astype(jnp.float32) * scale
        p = jax.nn.softmax(jnp.where(vis, s, NEG_INF), axis=-1)
        return jnp.einsum("bhqk,bhkd->bhqd", p.astype(v.dtype), v)

    return merge_heads(sweep_queries(attend, (q_nope, q_rope), pos))


def setup_inputs(seed: int = 0) -> dict:
    key = jax.random.key(seed)
    ks = jax.random.split(key, 32)
    f32 = jnp.float32

    def w(k, shape, fan_in):
        return jax.random.normal(k, shape, f32) * (fan_in ** -0.5)

    def gain(k, shape):
        return 1.0 + 0.02 * jax.random.normal(k, shape, f32)

    L = DEPTH
    return {
        "x": jax.random.normal(ks[0], (BATCH, SEQ, D_MODEL), f32),
        "meta_tokens": jax.random.normal(ks[1], (N_META, D_MODEL), f32),
        "rel_bias": 0.5 * jax.random.normal(ks[2], (REL_BUCKETS, DIFF_HEADS), f32),
        "norm_mix": gain(ks[3], (L, D_MODEL)),
        "w_in": w(ks[4], (L, D_MODEL, IN_COLS), D_MODEL),
        "conv_w": w(ks[5], (L, CONV_KERNEL, CONV_WIDTH), CONV_KERNEL),
        "conv_b": 0.01 * jax.random.normal(ks[6], (L, CONV_WIDTH), f32),
        "conv_ln_g": gain(ks[7], (L, CONV_WIDTH)),
        "conv_ln_b": 0.01 * jax.random.normal(ks[8], (L, CONV_WIDTH), f32),
        "w_br_conv": w(ks[9], (L, CONV_WIDTH, D_MODEL), CONV_WIDTH),
        "diff_lq1": 0.1 * jax.random.normal(ks[10], (L, DIFF_HEAD_DIM), f32),
        "diff_lk1": 0.1 * jax.random.normal(ks[11], (L, DIFF_HEAD_DIM), f32),
        "diff_lq2": 0.1 * jax.random.normal(ks[12], (L, DIFF_HEAD_DIM), f32),
        "diff_lk2": 0.1 * jax.random.normal(ks[13], (L, DIFF_HEAD_DIM), f32),
        "diff_subln": gain(ks[14], (L, DIFF_V_DIM)),
        "w_br_diff": w(ks[15], (L, DIFF_HEADS * DIFF_V_DIM, D_MODEL), DIFF_HEADS * DIFF_V_DIM),
        "mla_q_norm": gain(ks[16], (L, MLA_Q_RANK)),
        "w_uq": w(ks[17], (L, MLA_Q_RANK, MLA_HEADS * (MLA_NOPE_DIM + MLA_ROPE_DIM)), MLA_Q_RANK),
        "mla_kv_norm": gain(ks[18], (L, MLA_KV_RANK)),
        "w_ukv": w(ks[19], (L, MLA_KV_RANK, MLA_HEADS * (MLA_NOPE_DIM + MLA_V_DIM)), MLA_KV_RANK),
        "w_br_mla": w(ks[20], (L, MLA_HEADS * MLA_V_DIM, D_MODEL), MLA_HEADS * MLA_V_DIM),
        "w_out": w(ks[21], (L, D_MODEL, D_MODEL), D_MODEL),
        "norm_ffn": gain(ks[22], (L, D_MODEL)),
        "w_ffn_gate": w(ks[23], (L, D_MODEL, D_FF), D_MODEL),
        "w_ffn_up": w(ks[24], (L, D_MODEL, D_FF), D_MODEL),
        "w_ffn_down": w(ks[25], (L, D_FF, D_MODEL), D_FF),
        "final_norm": gain(ks[26], (D_MODEL,)),
    }


def reference(x, meta_tokens, rel_bias, norm_mix, w_in, conv_w, conv_b, conv_ln_g, conv_ln_b,
              w_br_conv, diff_lq1, diff_lk1, diff_lq2, diff_lk2, diff_subln, w_br_diff,
              mla_q_norm, w_uq, mla_kv_norm, w_ukv, w_br_mla, w_out, norm_ffn,
              w_ffn_gate, w_ffn_up, w_ffn_down, final_norm):
    b = x.shape[0]
    h = jnp.concatenate(
        [jnp.broadcast_to(meta_tokens[None].astype(x.dtype), (b, N_META, D_MODEL)), x], axis=1)
    L = h.shape[1]
    pos = jnp.arange(L, dtype=jnp.int32)
    for l in range(DEPTH):
        lam_init = 0.8 - 0.6 * math.exp(-0.3 * l)
        hn = rmsnorm(h, norm_mix[l])
        proj = hn @ w_in[l]
        ya = conv_branch(proj, conv_w[l], conv_b[l], conv_ln_g[l], conv_ln_b[l]) @ w_br_conv[l]
        yb = diff_branch(proj, pos, rel_bias, diff_lq1[l], diff_lk1[l], diff_lq2[l], diff_lk2[l],
                         diff_subln[l], lam_init) @ w_br_diff[l]
        yc = mla_branch(proj, pos, mla_q_norm[l], w_uq[l], mla_kv_norm[l], w_ukv[l]) @ w_br_mla[l]
        g = jax.nn.sigmoid(proj[..., OFF_G:]).reshape(b, L, N_BRANCH, D_MODEL)
        mixed = g[:, :, 0] * ya + g[:, :, 1] * yb + g[:, :, 2] * yc
        h = h + mixed @ w_out[l]
        hn = rmsnorm(h, norm_ffn[l])
        h = h + (jax.nn.silu(hn @ w_ffn_gate[l]) * (hn @ w_ffn_up[l])) @ w_ffn_down[l]
    return rmsnorm(h, final_norm)[:, N_META:]
```

```python
import contextlib
import math
import numpy as np
import concourse.bass as bass
import concourse.mybir as mybir
from concourse.bass_utils import run_bass_kernel_spmd

F32 = mybir.dt.float32
BF16 = mybir.dt.bfloat16
AF = mybir.ActivationFunctionType
ALU = mybir.AluOpType
AX = mybir.AxisListType

D_MODEL = 2048
N_META = 16
NORM_EPS = 1e-6
CONV_W = 1024
CONV_K = 31
OFF_A = 0
OFF_BQ = 2048
OFF_BK = 3072
OFF_BV = 4096
OFF_CQ = 5120
OFF_CKV = 5632
OFF_CR = 5888
OFF_G = 5952
IN_COLS = OFF_G + 3 * D_MODEL
D_FF = 5632
NEG = -30000.0
MLA_SCALE = 192.0 ** -0.5

LC_GMIX, LC_GFFN, LC_CW, LC_CB, LC_LNG, LC_LNB, LC_QNG, LC_KVG, LC_SUB, LC_LQ1, LC_LK1, LC_LQ2, LC_LK2 = \
    0, 16, 32, 280, 288, 296, 304, 308, 310, 438, 502, 566, 630
NLC = 694
GC_FIN, GC_B15, GC_ID, GC_MASK = 0, 16, 24, 152
NGC = 280

SAME_ENG_SYNC = True


class Eng:
    def __init__(self, key, obj, sem):
        self.key, self.obj, self.sem = key, obj, sem
        self.count = 0
        self.waited = {}


class SemOwner:
    def __init__(self, key, sem):
        self.key, self.sem = key, sem
        self.count = 0


class Slot:
    def __init__(self, t, name, space):
        self.t, self.name, self.space = t, name, space
        self.w = {}
        self.r = {}
        self.dsem = None

    def __getitem__(self, idx):
        return self.t[idx]


class Sched:
    def __init__(self, nc):
        self.nc = nc
        self.es = contextlib.ExitStack()
        self.eng = {}
        for key, obj in (("pe", nc.tensor), ("act", nc.scalar), ("dve", nc.vector),
                         ("pool", nc.gpsimd), ("sp", nc.sync)):
            sem = self.es.enter_context(nc.semaphore("sem_" + key))
            self.eng[key] = Eng(key, obj, sem)
        self.slots = []
        self.owners = []
        self.free_owners = []
        self.ninstr = 0
        self.scope = None
        self.scope_slots = []
        self.uid = 0

    def begin_scope(self):
        self.scope = contextlib.ExitStack()
        self.scope_slots = []

    def end_scope(self):
        self.barrier()
        self.scope.close()
        self.scope = None
        for s in self.scope_slots:
            self.slots.remove(s)
            if s.dsem is not None:
                self.free_owners.append(s.dsem)
                s.dsem = None
        self.scope_slots = []

    def _ctx(self):
        return self.scope if self.scope is not None else self.es

    def sbuf(self, name, shape, dtype, n=1):
        out = []
        for i in range(n):
            self.uid += 1
            nm = f"{name}_{i}_{self.uid}"
            t = self._ctx().enter_context(self.nc.sbuf_tensor(nm, list(shape), dtype))
            s = Slot(t, nm, "sbuf")
            self.slots.append(s)
            if self.scope is not None:
                self.scope_slots.append(s)
            out.append(s)
        return out if n > 1 else out[0]

    def psum(self, name, shape, dtype=F32):
        t = self.es.enter_context(self.nc.psum_tensor(name, list(shape), dtype))
        s = Slot(t, name, "psum")
        self.slots.append(s)
        return s

    def dram(self, name, shape, dtype, kind="Internal"):
        t = self.nc.dram_tensor(name, list(shape), dtype, kind=kind)
        s = Slot(t.ap(), name, "dram")
        self.slots.append(s)
        return s

    def _wait(self, E, dep):
        owner, val = dep
        if owner is E and (not SAME_ENG_SYNC or E.key == "pe"):
            return
        if E.waited.get(owner.key, 0) >= val:
            return
        E.obj.wait_ge(owner.sem, val)
        E.waited[owner.key] = val

    def _deps(self, E, reads, writes):
        for s in reads:
            for d in s.w.values():
                self._wait(E, d)
        for s in writes:
            for d in s.w.values():
                self._wait(E, d)
            for d in s.r.values():
                self._wait(E, d)

    def op(self, eng, fn, reads=(), writes=(), inc=True):
        E = self.eng[eng]
        self._deps(E, reads, writes)
        ins = fn(E.obj)
        self.ninstr += 1
        tgt = E.count + 1
        if inc:
            ins.then_inc(E.sem, 1)
            E.count = tgt
        for s in writes:
            s.w = {E.key: (E, tgt)}
            s.r = {}
        for s in reads:
            if s not in writes:
                s.r[E.key] = (E, tgt)
        return ins

    def dma(self, q, out_slot, out_ap, in_slot, in_ap):
        E = self.eng[q]
        self._deps(E, [in_slot], [out_slot])
        sl = out_slot if out_slot.space == "sbuf" else in_slot
        assert sl.space == "sbuf"
        if sl.dsem is None:
            if self.free_owners:
                sl.dsem = self.free_owners.pop()
            else:
                sem = self.es.enter_context(self.nc.semaphore(f"dsem{len(self.owners)}"))
                sl.dsem = SemOwner(f"d{len(self.owners)}", sem)
                self.owners.append(sl.dsem)
        ow = sl.dsem
        ins = E.obj.dma_start(out=out_ap, in_=in_ap)
        self.ninstr += 1
        ow.count += 16
        ins.then_inc(ow.sem, 16)
        dep = (ow, ow.count)
        if out_slot.space == "dram":
            out_slot.w[ow.key] = dep
        else:
            out_slot.w = {ow.key: dep}
            out_slot.r = {}
        in_slot.r[ow.key] = dep
        return ins

    def barrier(self):
        deps = [(E, E.count) for E in self.eng.values() if E.count > 0]
        deps += [(o, o.count) for o in self.owners if o.count > 0]
        for E in self.eng.values():
            for d in deps:
                self._wait(E, d)
        for s in self.slots:
            s.w = {}
            s.r = {}


class Rot:
    def __init__(self, items):
        self.items = list(items)
        self.i = 0

    def next(self):
        s = self.items[self.i % len(self.items)]
        self.i += 1
        return s


def subblocks(n, b=128):
    return [(s, min(b, n - s)) for s in range(0, n, b)]


def build(SEQ, DEPTH, TT, debug=False, stop=None):
    L = SEQ + N_META
    assert L % TT == 0 and SEQ % 128 == 0
    NT = L // TT
    NB = SEQ // 128
    n = TT
    nc = bass.Bass("TRN2", target_bir_lowering=False)
    S = Sched(nc)

    def ext(name, shape, dtype=F32):
        return Slot(nc.dram_tensor(name, list(shape), dtype, kind="ExternalInput").ap(), name, "dram")

    x_d = ext("x", [SEQ, D_MODEL])
    meta_d = ext("meta", [N_META, D_MODEL])
    w_in_d = ext("w_in", [DEPTH, D_MODEL, IN_COLS])
    w_crs_d = ext("w_crs", [DEPTH, D_MODEL, 64])
    w_uqn_d = ext("w_uqn", [DEPTH, 512, 1024])
    w_uqr_d = ext("w_uqr", [DEPTH, 512, 1024])
    w_kn_d = ext("w_kn", [DEPTH, 256, 1024])
    w_kv_d = ext("w_kv", [DEPTH, 256, 1024])
    w_brc_d = ext("w_br_conv", [DEPTH, 1024, D_MODEL])
    w_brd_d = ext("w_br_diff", [DEPTH, 1024, D_MODEL])
    w_brm_d = ext("w_br_mla", [DEPTH, 1024, D_MODEL])
    w_out_d = ext("w_out", [DEPTH, D_MODEL, D_MODEL])
    w_fg_d = ext("w_ffn_gate", [DEPTH, D_MODEL, D_FF])
    w_fu_d = ext("w_ffn_up", [DEPTH, D_MODEL, D_FF])
    w_fd_d = ext("w_ffn_down", [DEPTH, D_FF, D_MODEL])
    lc_d = ext("lc", [DEPTH, 128, NLC])
    gc_d = ext("gc", [128, NGC])
    bias_d = ext("biasT", [128, 8 * 4 * 128])
    ropeC_d = ext("ropeC", [128, L])
    ropeS_d = ext("ropeS", [128, L])
    out_t = nc.dram_tensor("out", [SEQ, D_MODEL], F32, kind="ExternalOutput")
    out_d = Slot(out_t.ap(), "out", "dram")

    dk = "ExternalOutput" if debug else "Internal"
    hT = [S.dram(f"hT{i}", [D_MODEL, L], F32, kind=dk) for i in range(2)]
    csT = S.dram("csT", [1024, L], BF16, kind=dk)
    QdT = S.dram("QdT", [1024, L], BF16, kind=dk)
    KdT = S.dram("KdT", [1024, L], BF16, kind=dk)
    Vd = S.dram("Vd", [L, 1024], BF16, kind=dk)
    QnT = S.dram("QnT", [1024, L], BF16, kind=dk)
    QrT = S.dram("QrT", [512, L], BF16, kind=dk)
    KnT = S.dram("KnT", [1024, L], BF16, kind=dk)
    KrT = S.dram("KrT", [64, L], BF16, kind=dk)
    Vm = S.dram("Vm", [L, 1024], BF16, kind=dk)
    G = S.dram("G", [3 * D_MODEL, L], F32, kind=dk)
    oTd = S.dram("oTd", [1024, L], BF16, kind=dk)
    oTm = S.dram("oTm", [1024, L], BF16, kind=dk)

    gc = S.sbuf("gc", [128, NGC], F32)
    lc = S.sbuf("lc", [128, NLC], F32)
    ones_bf = S.sbuf("ones_bf", [128, 128], BF16)
    ones_f = S.sbuf("ones_f", [128, 128], F32)
    lamv = S.sbuf("lamv", [128, 8], F32)
    gsub = S.sbuf("gsub", [128, 128], F32)
    junk = S.sbuf("junk", [128, 128], F32)
    WSLOTS = S.sbuf("w", [128, 16 * 512], BF16, n=3)
    wrot = Rot(WSLOTS)
    PS = [S.psum(f"ps{i}", [128, 512]) for i in range(8)]

    identf = lambda a, b: gc[0:a, GC_ID:GC_ID + b]


    S.dma("sp", gc, gc[:, :], gc_d, gc_d[:, :])
    S.op("pool", lambda e: e.memset(ones_bf[:, :], 1.0), writes=[ones_bf])
    S.op("pool", lambda e: e.memset(ones_f[:, :], 1.0), writes=[ones_f])

    def run_jobs(jobs, pref=2):
        loaded = {}

        def ensure(i):
            if i < len(jobs) and jobs[i][0] is not None and i not in loaded:
                src_slot, src_ap, kc, ncols = jobs[i][0]
                slot = wrot.next()
                view = slot[:, 0:kc * ncols].rearrange("p (c n) -> p c n", n=ncols)
                S.dma("pool", slot, view, src_slot, src_ap)
                loaded[i] = (slot, view)

        for i in range(len(jobs)):
            for j in range(i, min(i + pref + 1, len(jobs))):
                ensure(j)
            jobs[i][1](*(loaded.pop(i) if i in loaded else (None, None)))

    def wspec(wd, l, r0, kc, c0, ncols):
        ap = wd[l, r0:r0 + kc * 128, c0:c0 + ncols].rearrange("(c p) n -> p c n", p=128)
        return (wd, ap, kc, ncols)

    def rmsnorm_fm(x_slot, xv, nch, gain, Dn, out_slot, ov, sqrot, rstd_slot, psb, nn, pp=128):
        for c in range(nch):
            sq = sqrot.next()
            S.op("act", lambda e, c=c, sq=sq: e.activation(out=sq[0:pp, 0:nn], in_=xv(c), func=AF.Square),
                 reads=[x_slot], writes=[sq])
            S.op("pe", lambda e, c=c, sq=sq: e.matmul(psb[:, 0:nn], ones_bf[0:pp, :], sq[0:pp, 0:nn],
                                                      start=(c == 0), stop=(c == nch - 1)),
                 reads=[sq, ones_bf], writes=[psb], inc=True)
        S.op("dve", lambda e: e.tensor_scalar(out=rstd_slot[:, 0:nn], in0=psb[:, 0:nn], scalar1=1.0 / Dn,
                                              scalar2=NORM_EPS, op0=ALU.mult, op1=ALU.add),
             reads=[psb], writes=[rstd_slot])
        S.op("act", lambda e: e.activation(out=rstd_slot[:, 0:nn], in_=rstd_slot[:, 0:nn], func=AF.Sqrt),
             reads=[rstd_slot], writes=[rstd_slot])
        S.op("dve", lambda e: e.reciprocal(out=rstd_slot[:, 0:nn], in_=rstd_slot[:, 0:nn]),
             reads=[rstd_slot], writes=[rstd_slot])
        for c in range(nch):
            S.op("dve", lambda e, c=c: e.scalar_tensor_tensor(out=ov(c), in0=xv(c), scalar=gain(c),
                                                              in1=rstd_slot[0:pp, 0:nn], op0=ALU.mult, op1=ALU.mult),
                 reads=[x_slot, rstd_slot, lc, gc], writes=[out_slot])

    def fm_mm(view, wslot, kc, m, msz, rhs_slot, rhs, ps, nn, start=True, stop=True):
        for c in range(kc):
            S.op("pe", lambda e, c=c: e.matmul(ps[0:msz, 0:nn], view[:, c, m * 128:m * 128 + msz], rhs(c),
                                               start=(start and c == 0), stop=(stop and c == kc - 1)),
                 reads=[wslot, rhs_slot], writes=[ps], inc=(c == kc - 1))

    S.begin_scope()
    xt = S.sbuf("xt", [128, D_MODEL], F32, n=2)
    xrot = Rot(xt)
    stg = S.sbuf("pstg", [128, 16, 128], F32, n=2)
    srot = Rot(stg)
    psr = Rot(PS[0:4])
    for (p0, rows) in subblocks(L):
        t = xrot.next()
        if p0 == 0:
            S.dma("sp", t, t[0:N_META, :], meta_d, meta_d[:, :])
            S.dma("sp", t, t[N_META:rows, :], x_d, x_d[0:rows - N_META, :])
        else:
            S.dma("sp", t, t[0:rows, :], x_d, x_d[p0 - N_META:p0 - N_META + rows, :])
        st = srot.next()
        for g4 in range(4):
            ps = psr.next()
            for q in range(4):
                c = g4 * 4 + q
                S.op("pe", lambda e, c=c, q=q, ps=ps, t=t: e.transpose(ps[:, q * 128:q * 128 + rows],
                                                                     t[0:rows, c * 128:(c + 1) * 128],
                                                                     identf(rows, rows)),
                     reads=[t, gc], writes=[ps], inc=(q == 3))
            S.op("dve" if g4 % 2 == 0 else "act",
                 (lambda e, ps=ps, st=st, g4=g4: e.tensor_copy(
                     st[:, g4 * 4:g4 * 4 + 4, 0:rows],
                     ps[:, :].rearrange("p (q t) -> p q t", t=128)[:, :, 0:rows])) if g4 % 2 == 0 else
                 (lambda e, ps=ps, st=st, g4=g4: e.activation(
                     out=st[:, g4 * 4:g4 * 4 + 4, 0:rows],
                     in_=ps[:, :].rearrange("p (q t) -> p q t", t=128)[:, :, 0:rows], func=AF.Copy)),
                 reads=[ps], writes=[st])
        S.dma("sp", hT[0], hT[0][:, p0:p0 + rows].rearrange("(c p) t -> p c t", p=128), st, st[:, :, 0:rows])
    S.end_scope()
    if stop == "pro":
        S.barrier()
        return nc, S

    for l in range(DEPTH):
        lam_init = 0.8 - 0.6 * math.exp(-0.3 * l)
        h_in, h_out = hT[l % 2], hT[(l + 1) % 2]
        S.dma("sp", lc, lc[:, :], lc_d, lc_d[l, :, :])
        S.op("dve", lambda e: e.tensor_tensor(out=junk[:, 0:64], in0=lc[:, LC_LQ1:LC_LQ1 + 64],
                                              in1=lc[:, LC_LK1:LC_LK1 + 64], op=ALU.mult), reads=[lc], writes=[junk])
        S.op("dve", lambda e: e.tensor_reduce(out=lamv[:, 0:1], in_=junk[:, 0:64], axis=AX.X, op=ALU.add),
             reads=[junk], writes=[lamv])
        S.op("dve", lambda e: e.tensor_tensor(out=junk[:, 0:64], in0=lc[:, LC_LQ2:LC_LQ2 + 64],
                                              in1=lc[:, LC_LK2:LC_LK2 + 64], op=ALU.mult), reads=[lc], writes=[junk])
        S.op("dve", lambda e: e.tensor_reduce(out=lamv[:, 1:2], in_=junk[:, 0:64], axis=AX.X, op=ALU.add),
             reads=[junk, lamv], writes=[lamv])
        S.op("act", lambda e: e.activation(out=lamv[:, 2:4], in_=lamv[:, 0:2], func=AF.Exp), reads=[lamv], writes=[lamv])
        S.op("dve", lambda e: e.tensor_tensor(out=lamv[:, 4:5], in0=lamv[:, 2:3], in1=lamv[:, 3:4], op=ALU.subtract),
             reads=[lamv], writes=[lamv])
        S.op("dve", lambda e: e.tensor_scalar(out=lamv[:, 5:6], in0=lamv[:, 4:5], scalar1=lam_init, scalar2=-1.0,
                                              op0=ALU.add, op1=ALU.mult), reads=[lamv], writes=[lamv])
        S.op("dve", lambda e: e.tensor_scalar(out=gsub[:, :], in0=lc[:, LC_SUB:LC_SUB + 128], scalar1=1.0 - lam_init,
                                              scalar2=None, op0=ALU.mult), reads=[lc], writes=[gsub])

        S.begin_scope()
        h_sb = S.sbuf("h", [128, 16, n], F32)
        sqrot = Rot(S.sbuf("sq", [128, n], BF16, n=3))
        rstd = S.sbuf("rstd", [128, n], F32)
        hn = S.sbuf("hn", [128, 16, n], BF16)
        a_buf = S.sbuf("abuf", [128, 8, 32 + n], F32)
        acc = S.sbuf("acc", [128, 8, n], F32)
        tfrot = Rot(S.sbuf("tf", [128, n], F32, n=3))
        cs = S.sbuf("cs", [128, 8, n], BF16)
        cq = S.sbuf("cq", [128, 4, n], F32)
        cqn = S.sbuf("cqn", [128, 4, n], BF16)
        ckv = S.sbuf("ckv", [128, 2, n], F32)
        ckvn = S.sbuf("ckvn", [128, 2, n], BF16)
        kr = S.sbuf("kr", [128, 2, n], F32)
        sbrot = Rot(S.sbuf("sb", [128, n], BF16, n=4))
        sfrot = Rot(S.sbuf("sf", [128, n], F32, n=3))
        svrot = Rot(S.sbuf("sv", [128, 512], BF16, n=3))
        rC = S.sbuf("rC", [128, n], F32)
        rS = S.sbuf("rS", [128, n], F32)
        psr = Rot(PS[0:6])
        ps_s1, ps_s2 = PS[6], PS[7]
        S.op("pool", lambda e: e.memset(a_buf[:, :, 0:32], 0.0), writes=[a_buf])
        jobs = []
        for t in range(NT):
            p0 = t * n
            tsb = subblocks(n)

            def j_norm(ws, wv, p0=p0):
                S.dma("sp", h_sb, h_sb[:, :, :], h_in, h_in[:, p0:p0 + n].rearrange("(c p) t -> p c t", p=128))
                S.dma("sp", rC, rC[:, :], ropeC_d, ropeC_d[:, p0:p0 + n])
                S.dma("sp", rS, rS[:, :], ropeS_d, ropeS_d[:, p0:p0 + n])
                rmsnorm_fm(h_sb, lambda c: h_sb[:, c, :], 16, lambda c: lc[:, LC_GMIX + c:LC_GMIX + c + 1], D_MODEL,
                           hn, lambda c: hn[:, c, :], sqrot, rstd, ps_s1, n)
            jobs.append((None, j_norm))

            hn_rhs = lambda c: hn[:, c, :]

            def mk_fm(handler, nchunk, chunk_sizes=None):
                def job(ws, wv):
                    for m in range(nchunk):
                        msz = 128 if chunk_sizes is None else chunk_sizes[m]
                        ps = psr.next()
                        fm_mm(wv, ws, 16, m, msz, hn, hn_rhs, ps, n)
                        handler(m, ps)
                return job

            for g in range(2):
                def h_gate(m, ps, g=g):
                    gm = g * 4 + m
                    S.op("act", lambda e: e.activation(out=a_buf[:, gm, 32:32 + n], in_=ps[:, 0:n], func=AF.Sigmoid),
                         reads=[ps], writes=[a_buf])
                jobs.append((wspec(w_in_d, l, 0, 16, OFF_A + CONV_W + g * 512, 512), mk_fm(h_gate, 4)))
            for g in range(2):
                def h_val(m, ps, g=g):
                    gm = g * 4 + m
                    S.op("dve", lambda e: e.tensor_tensor(out=a_buf[:, gm, 32:32 + n], in0=ps[:, 0:n],
                                                          in1=a_buf[:, gm, 32:32 + n], op=ALU.mult),
                         reads=[ps, a_buf], writes=[a_buf])
                jobs.append((wspec(w_in_d, l, 0, 16, OFF_A + g * 512, 512), mk_fm(h_val, 4)))

            def j_conv(ws, wv, p0=p0):
                for c in range(8):
                    S.op("dve", lambda e, c=c: e.tensor_scalar(out=acc[:, c, :], in0=a_buf[:, c, 2:2 + n],
                                                               scalar1=lc[:, LC_CW + c * 31:LC_CW + c * 31 + 1],
                                                               scalar2=lc[:, LC_CB + c:LC_CB + c + 1],
                                                               op0=ALU.mult, op1=ALU.add),
                         reads=[a_buf, lc], writes=[acc])
                    for j in range(1, CONV_K):
                        S.op("dve", lambda e, c=c, j=j: e.scalar_tensor_tensor(
                            out=acc[:, c, :], in0=a_buf[:, c, 2 + j:2 + j + n],
                            scalar=lc[:, LC_CW + c * 31 + j:LC_CW + c * 31 + j + 1], in1=acc[:, c, :],
                            op0=ALU.mult, op1=ALU.add), reads=[a_buf, lc, acc], writes=[acc])
                S.op("pool", lambda e: e.tensor_copy(a_buf[:, :, 0:32], a_buf[:, :, n:n + 32]), reads=[a_buf], writes=[a_buf])
                for c in range(8):
                    tf = tfrot.next()
                    S.op("act", lambda e, c=c, tf=tf: e.activation(out=tf[:, :], in_=acc[:, c, :], func=AF.Square),
                         reads=[acc], writes=[tf])
                    S.op("pe", lambda e, c=c: e.matmul(ps_s1[:, 0:n], ones_f[:, :], acc[:, c, :], start=(c == 0), stop=(c == 7)),
                         reads=[acc, ones_f], writes=[ps_s1], inc=(c == 7))
                    S.op("pe", lambda e, c=c, tf=tf: e.matmul(ps_s2[:, 0:n], ones_f[:, :], tf[:, :], start=(c == 0), stop=(c == 7)),
                         reads=[tf, ones_f], writes=[ps_s2], inc=True)
                mean = tfrot.next()
                S.op("dve", lambda e: e.tensor_scalar(out=mean[:, :], in0=ps_s1[:, 0:n], scalar1=1.0 / CONV_W, scalar2=None,
                                                      op0=ALU.mult), reads=[ps_s1], writes=[mean])
                msq = tfrot.next()
                S.op("dve", lambda e: e.tensor_tensor(out=msq[:, :], in0=mean[:, :], in1=mean[:, :], op=ALU.mult),
                     reads=[mean], writes=[msq])
                S.op("dve", lambda e: e.scalar_tensor_tensor(out=msq[:, :], in0=ps_s2[:, 0:n], scalar=1.0 / CONV_W,
                                                             in1=msq[:, :], op0=ALU.mult, op1=ALU.subtract),
                     reads=[ps_s2, msq], writes=[msq])
                S.op("dve", lambda e: e.tensor_scalar(out=msq[:, :], in0=msq[:, :], scalar1=NORM_EPS, scalar2=None,
                                                      op0=ALU.add), reads=[msq], writes=[msq])
                S.op("act", lambda e: e.activation(out=msq[:, :], in_=msq[:, :], func=AF.Sqrt), reads=[msq], writes=[msq])
                S.op("dve", lambda e: e.reciprocal(out=msq[:, :], in_=msq[:, :]), reads=[msq], writes=[msq])
                for c in range(8):
                    S.op("dve", lambda e, c=c: e.tensor_tensor(out=acc[:, c, :], in0=acc[:, c, :], in1=mean[:, :],
                                                               op=ALU.subtract), reads=[acc, mean], writes=[acc])
                    S.op("dve", lambda e, c=c: e.tensor_tensor(out=acc[:, c, :], in0=acc[:, c, :], in1=msq[:, :],
                                                               op=ALU.mult), reads=[acc, msq], writes=[acc])
                    S.op("act", lambda e, c=c: e.activation(out=cs[:, c, :], in_=acc[:, c, :], func=AF.Silu,
                                                            scale=lc[:, LC_LNG + c:LC_LNG + c + 1],
                                                            bias=lc[:, LC_LNB + c:LC_LNB + c + 1]),
                         reads=[acc, lc], writes=[cs])
                S.dma("sp", csT, csT[:, p0:p0 + n].rearrange("(c p) t -> p c t", p=128), cs, cs[:, :, :])
            jobs.append((None, j_conv))

            for g in range(2):
                def h_bq(m, ps, g=g, p0=p0):
                    gm = g * 4 + m
                    sb = sbrot.next()
                    S.op("act", lambda e: e.activation(out=sb[:, :], in_=ps[:, 0:n], func=AF.Copy, scale=0.125),
                         reads=[ps], writes=[sb])
                    S.dma("sp", QdT, QdT[gm * 128:(gm + 1) * 128, p0:p0 + n], sb, sb[:, :])
                jobs.append((wspec(w_in_d, l, 0, 16, OFF_BQ + g * 512, 512), mk_fm(h_bq, 4)))
            for g in range(2):
                def h_bk(m, ps, g=g, p0=p0):
                    gm = g * 4 + m
                    sb = sbrot.next()
                    S.op("dve", lambda e: e.tensor_copy(sb[:, :], ps[:, 0:n]), reads=[ps], writes=[sb])
                    S.dma("sp", KdT, KdT[gm * 128:(gm + 1) * 128, p0:p0 + n], sb, sb[:, :])
                jobs.append((wspec(w_in_d, l, 0, 16, OFF_BK + g * 512, 512), mk_fm(h_bk, 4)))

            def mk_tm(act_slot, actv, kc, dst, colbase, p0=p0, tsb=tsb):
                def job(ws, wv):
                    ncols = 512
                    for (s0, sz) in tsb:
                        ps = psr.next()
                        for c in range(kc):
                            S.op("pe", lambda e, c=c: e.matmul(ps[0:sz, 0:ncols], actv(c, s0, sz), wv[:, c, 0:ncols],
                                                               start=(c == 0), stop=(c == kc - 1)),
                                 reads=[ws, act_slot], writes=[ps], inc=(c == kc - 1))
                        sv = svrot.next()
                        S.op("dve", lambda e: e.tensor_copy(sv[0:sz, :], ps[0:sz, 0:ncols]), reads=[ps], writes=[sv])
                        S.dma("sp", dst, dst[p0 + s0:p0 + s0 + sz, colbase:colbase + ncols], sv, sv[0:sz, :])
                return job
            for g in range(2):
                jobs.append((wspec(w_in_d, l, 0, 16, OFF_BV + g * 512, 512),
                             mk_tm(hn, lambda c, s0, sz: hn[:, c, s0:s0 + sz], 16, Vd, g * 512)))

            def h_cq(m, ps):
                S.op("dve", lambda e: e.tensor_copy(cq[:, m, :], ps[:, 0:n]), reads=[ps], writes=[cq])
            jobs.append((wspec(w_in_d, l, 0, 16, OFF_CQ, 512), mk_fm(h_cq, 4)))

            def h_ckv(m, ps):
                if m < 2:
                    S.op("dve", lambda e: e.tensor_copy(ckv[:, m, :], ps[:, 0:n]), reads=[ps], writes=[ckv])
                else:
                    S.op("dve", lambda e: e.tensor_copy(kr[0:64, 0, :], ps[0:64, 0:n]), reads=[ps], writes=[kr])
            jobs.append((wspec(w_in_d, l, 0, 16, OFF_CKV, 320), mk_fm(h_ckv, 3, [128, 128, 64])))

            def h_crs(m, ps, p0=p0):
                S.op("dve", lambda e: e.tensor_copy(kr[0:64, 1, :], ps[0:64, 0:n]), reads=[ps], writes=[kr])
                t1 = sfrot.next()
                S.op("dve", lambda e: e.tensor_tensor(out=t1[0:64, :], in0=kr[0:64, 0, :], in1=rC[0:64, :], op=ALU.mult),
                     reads=[kr, rC], writes=[t1])
                t2 = sfrot.next()
                S.op("dve", lambda e: e.tensor_tensor(out=t2[0:64, :], in0=kr[0:64, 1, :], in1=rS[0:64, :], op=ALU.mult),
                     reads=[kr, rS], writes=[t2])
                sb = sbrot.next()
                S.op("dve", lambda e: e.tensor_tensor(out=sb[0:64, :], in0=t1[0:64, :], in1=t2[0:64, :], op=ALU.add),
                     reads=[t1, t2], writes=[sb])
                S.dma("sp", KrT, KrT[:, p0:p0 + n], sb, sb[0:64, :])
            jobs.append((wspec(w_crs_d, l, 0, 16, 0, 64), mk_fm(h_crs, 1, [64])))

            def j_qnorm(ws, wv):
                rmsnorm_fm(cq, lambda c: cq[:, c, :], 4, lambda c: lc[:, LC_QNG + c:LC_QNG + c + 1], 512,
                           cqn, lambda c: cqn[:, c, :], sqrot, rstd, ps_s1, n)
            jobs.append((None, j_qnorm))
            cqn_rhs = lambda c: cqn[:, c, :]
            for g in range(2):
                def j_qn(ws, wv, g=g, p0=p0):
                    for m in range(4):
                        gm = g * 4 + m
                        ps = psr.next()
                        fm_mm(wv, ws, 4, m, 128, cqn, cqn_rhs, ps, n)
                        sb = sbrot.next()
                        S.op("act", lambda e: e.activation(out=sb[:, :], in_=ps[:, 0:n], func=AF.Copy, scale=MLA_SCALE),
                             reads=[ps], writes=[sb])
                        S.dma("sp", QnT, QnT[gm * 128:(gm + 1) * 128, p0:p0 + n], sb, sb[:, :])
                jobs.append((wspec(w_uqn_d, l, 0, 4, g * 512, 512), j_qn))

            def j_qr(ws, wv, p0=p0):
                for m in range(4):
                    psa = psr.next()
                    fm_mm(wv, ws, 4, m, 128, cqn, cqn_rhs, psa, n)
                    psb = psr.next()
                    fm_mm(wv, ws, 4, 4 + m, 128, cqn, cqn_rhs, psb, n)
                    t1 = sfrot.next()
                    S.op("dve", lambda e: e.scalar_tensor_tensor(out=t1[:, :], in0=psa[:, 0:n], scalar=MLA_SCALE,
                                                                 in1=rC[:, :], op0=ALU.mult, op1=ALU.mult),
                         reads=[psa, rC], writes=[t1])
                    t2 = sfrot.next()
                    S.op("dve", lambda e: e.scalar_tensor_tensor(out=t2[:, :], in0=psb[:, 0:n], scalar=MLA_SCALE,
                                                                 in1=rS[:, :], op0=ALU.mult, op1=ALU.mult),
                         reads=[psb, rS], writes=[t2])
                    sb = sbrot.next()
                    S.op("dve", lambda e: e.tensor_tensor(out=sb[:, :], in0=t1[:, :], in1=t2[:, :], op=ALU.add),
                         reads=[t1, t2], writes=[sb])
                    S.dma("sp", QrT, QrT[m * 128:(m + 1) * 128, p0:p0 + n], sb, sb[:, :])
            jobs.append((wspec(w_uqr_d, l, 0, 4, 0, 1024), j_qr))

            def j_kvnorm(ws, wv):
                rmsnorm_fm(ckv, lambda c: ckv[:, c, :], 2, lambda c: lc[:, LC_KVG + c:LC_KVG + c + 1], 256,
                           ckvn, lambda c: ckvn[:, c, :], sqrot, rstd, ps_s1, n)
            jobs.append((None, j_kvnorm))
            ckvn_rhs = lambda c: ckvn[:, c, :]

            def j_kn(ws, wv, p0=p0):
                for m in range(8):
                    ps = psr.next()
                    fm_mm(wv, ws, 2, m, 128, ckvn, ckvn_rhs, ps, n)
                    sb = sbrot.next()
                    S.op("dve" if m % 2 else "act",
                         (lambda e: e.tensor_copy(sb[:, :], ps[:, 0:n])) if m % 2 else
                         (lambda e: e.activation(out=sb[:, :], in_=ps[:, 0:n], func=AF.Copy)),
                         reads=[ps], writes=[sb])
                    S.dma("sp", KnT, KnT[m * 128:(m + 1) * 128, p0:p0 + n], sb, sb[:, :])
            jobs.append((wspec(w_kn_d, l, 0, 2, 0, 1024), j_kn))

            def j_kvv(ws, wv, p0=p0, tsb=tsb):
                for (s0, sz) in tsb:
                    for half in range(2):
                        ps = psr.next()
                        for c in range(2):
                            S.op("pe", lambda e, c=c: e.matmul(ps[0:sz, 0:512], ckvn[:, c, s0:s0 + sz],
                                                               wv[:, c, half * 512:(half + 1) * 512],
                                                               start=(c == 0), stop=(c == 1)),
                                 reads=[ws, ckvn], writes=[ps], inc=(c == 1))
                        sv = svrot.next()
                        S.op("dve", lambda e: e.tensor_copy(sv[0:sz, :], ps[0:sz, 0:512]), reads=[ps], writes=[sv])
                        S.dma("sp", Vm, Vm[p0 + s0:p0 + s0 + sz, half * 512:(half + 1) * 512], sv, sv[0:sz, :])
            jobs.append((wspec(w_kv_d, l, 0, 2, 0, 1024), j_kvv))

            for g in range(12):
                def h_g(m, ps, g=g, p0=p0):
                    gm = g * 4 + m
                    sf = sfrot.next()
                    S.op("act", lambda e: e.activation(out=sf[:, :], in_=ps[:, 0:n], func=AF.Sigmoid),
                         reads=[ps], writes=[sf])
                    S.dma("sp", G, G[gm * 128:(gm + 1) * 128, p0:p0 + n], sf, sf[:, :])
                jobs.append((wspec(w_in_d, l, 0, 16, OFF_G + g * 512, 512), mk_fm(h_g, 4)))
        run_jobs(jobs)
        S.end_scope()
        if stop == "A":
            S.barrier()
            return nc, S

        def attention(kind):
            S.begin_scope()
            diff = (kind == "diff")
            nsm = 2 if diff else 1
            Kt = S.sbuf("Kt", [128, L], BF16, n=2)
            Vs = S.sbuf("Vs", [128, NB + 1, 129], BF16, n=2)
            for v in Vs:
                S.op("pool", lambda e, v=v: e.memset(v[:, :, 128:129], 1.0), writes=[v])
            Qt = S.sbuf("Qt", [128, 512], BF16, n=2)
            qrot = Rot(Qt)
            if not diff:
                Krt = S.sbuf("Krt", [64, L], BF16)
                S.dma("sp", Krt, Krt[:, :], KrT, KrT[:, :])
                Qr = S.sbuf("Qr", [64, 512], BF16, n=2)
                qrrot = Rot(Qr)
            Pt = [Rot(S.sbuf(f"Pt{s}", [128, 512], BF16, n=3)) for s in range(nsm)]
            smrot = Rot(S.sbuf("sm", [128, 8], F32, n=4))
            ofrot = Rot(S.sbuf("of", [128, 128], F32, n=3))
            onrot = Rot(S.sbuf("on", [128, 128], F32, n=3))
            oTrot = Rot(S.sbuf("oT", [128, 512], BF16, n=2))
            sjunk = S.sbuf("sjunk", [128, 128], F32)
            psS = Rot(PS[0:4])
            if diff:
                Obanks = [[PS[4], PS[5]], [PS[6], PS[7]]]
                orot = None
            else:
                orot = Rot([[PS[4], PS[5]], [PS[6], PS[7]]])
            QT_src = QdT if diff else QnT
            KT_src = KdT if diff else KnT
            V_src = Vd if diff else Vm
            O_dst = oTd if diff else oTm
            qtiles = [(0, [N_META], True)]
            for i in range((NB + 3) // 4):
                nsub = min(4, NB - 4 * i)
                qtiles.append((N_META + 512 * i, [128] * nsub, False))
            bias2 = S.sbuf("bias", [128, 512], F32, n=2)
            for h in range(8):
                K = Kt[h % 2]
                V = Vs[h % 2]
                bias_sb = bias2[h % 2]
                if diff:
                    S.dma("sp", bias_sb, bias_sb[:, :], bias_d, bias_d[:, h * 512:(h + 1) * 512])
                bias_tile = lambda h_, kind, nk_, ncols: bias_sb[0:nk_, kind * 128:kind * 128 + ncols]
                S.dma("sp", K, K[:, :], KT_src, KT_src[h * 128:(h + 1) * 128, :])
                S.dma("sp", V, V[0:N_META, 0, 0:128], V_src, V_src[0:N_META, h * 128:(h + 1) * 128])
                S.dma("sp", V, V[:, 1:NB + 1, 0:128], V_src,
                      V_src[N_META:L, h * 128:(h + 1) * 128].rearrange("(j p) d -> p j d", p=128))
                for ti, (qp0, sizes, is_meta) in enumerate(qtiles):
                    nq = sum(sizes)
                    nsub = len(sizes)
                    i = ti - 1
                    Q = qrot.next()
                    S.dma("sp", Q, Q[:, 0:nq], QT_src, QT_src[h * 128:(h + 1) * 128, qp0:qp0 + nq])
                    if not diff:
                        Qrs = qrrot.next()
                        S.dma("sp", Qrs, Qrs[:, 0:nq], QrT, QrT[h * 64:(h + 1) * 64, qp0:qp0 + nq])
                    Ob = Obanks if diff else [orot.next()]
                    O = lambda s, r: Ob[s][r // 2][:, (r % 2) * 256:(r % 2) * 256 + 129]
                    Oslot = lambda s, r: Ob[s][r // 2]
                    kbs = [(0, N_META, 0, -1)]
                    if not is_meta:
                        for j in range(4 * i + nsub):
                            kbs.append((N_META + 128 * j, 128, j + 1, j))
                    for (kc0, nk, vblk, j) in kbs:
                        r0 = 0 if j < 0 else max(0, j - 4 * i)
                        specials = {}
                        if is_meta:
                            specials[0] = 3
                        elif j < 0:
                            if i == 0 and diff:
                                specials[0] = 2
                        else:
                            rd = j - 4 * i
                            if 0 <= rd < nsub:
                                specials[rd] = 0
                            if diff and 0 <= rd + 1 < nsub:
                                specials[rd + 1] = 1
                        qsz = sizes[0]
                        c_lo = r0 * 128
                        sp_hi = (max(specials) + 1) * 128 if specials else c_lo
                        if is_meta:
                            sp_hi = qsz
                        for s in range(nsm):
                            ps = psS.next()
                            P = Pt[s].next()

                            def score(c0, c1, first):
                                if diff:
                                    S.op("pe", lambda e: e.matmul(ps[0:nk, c0:c1], K[s * 64:(s + 1) * 64, kc0:kc0 + nk],
                                                                  Q[s * 64:(s + 1) * 64, c0:c1], start=first, stop=True),
                                         reads=[K, Q], writes=[ps], inc=True)
                                else:
                                    S.op("pe", lambda e: e.matmul(ps[0:nk, c0:c1], K[:, kc0:kc0 + nk], Q[:, c0:c1],
                                                                  start=first, stop=False),
                                         reads=[K, Q], writes=[ps], inc=False)
                                    S.op("pe", lambda e: e.matmul(ps[0:nk, c0:c1], Krt[0:64, kc0:kc0 + nk], Qrs[0:64, c0:c1],
                                                                  start=False, stop=True),
                                         reads=[Krt, Qrs], writes=[ps], inc=True)
                            for r, knd in sorted(specials.items()):
                                c0 = r * 128
                                c1 = c0 + qsz
                                bt = bias_tile(h, knd, nk, qsz) if diff else gc[0:nk, GC_MASK:GC_MASK + qsz]
                                S.op("pe", lambda e, c0=c0, c1=c1, bt=bt: e.matmul(ps[0:nk, c0:c1], identf(nk, nk), bt,
                                                                                    start=True, stop=False),
                                     reads=[gc, bias_sb], writes=[ps], inc=False)
                                score(c0, c1, False)
                            rest_lo = c_lo
                            done_cols = sorted(specials)
                            segs = []
                            cur = c_lo
                            for r in done_cols:
                                if r * 128 > cur:
                                    segs.append((cur, r * 128))
                                cur = r * 128 + qsz
                            if cur < nq:
                                segs.append((cur, nq))
                            for (c0, c1) in segs:
                                score(c0, c1, True)
                            for r in done_cols:
                                c0 = r * 128
                                S.op("act", lambda e, c0=c0: e.activation(out=P[0:nk, c0:c0 + qsz], in_=ps[0:nk, c0:c0 + qsz],
                                                                          func=AF.Exp), reads=[ps], writes=[P])
                            for (c0, c1) in segs:
                                if diff:
                                    S.op("act", lambda e, c0=c0, c1=c1: e.activation(
                                        out=P[0:nk, c0:c1], in_=ps[0:nk, c0:c1], func=AF.Exp,
                                        bias=gc[0:nk, GC_B15 + h:GC_B15 + h + 1]), reads=[ps, gc], writes=[P])
                                else:
                                    S.op("act", lambda e, c0=c0, c1=c1: e.activation(
                                        out=P[0:nk, c0:c1], in_=ps[0:nk, c0:c1], func=AF.Exp), reads=[ps], writes=[P])
                            rr = list(range(r0, nsub))
                            for r in rr:
                                last_kb = is_meta or (j == 4 * i + r)
                                S.op("pe", lambda e, r=r, last_kb=last_kb: e.matmul(
                                    O(s, r)[0:qsz, :], P[0:nk, r * 128:r * 128 + qsz], V[0:nk, vblk, 0:129],
                                    start=(j < 0 and r % 2 == 0), stop=last_kb, skip_group_check=True),
                                    reads=[P, V], writes=[Oslot(s, r)], inc=(r == rr[-1]))
                    oT = oTrot.next()
                    for r in range(nsub):
                        qsz = sizes[r]
                        sm = smrot.next()
                        of = ofrot.next()
                        on = onrot.next()
                        if diff:
                            S.op("dve", lambda e: e.reciprocal(out=sm[0:qsz, 0:1], in_=O(0, r)[0:qsz, 128:129]),
                                 reads=[Oslot(0, r)], writes=[sm])
                            S.op("dve", lambda e: e.reciprocal(out=sm[0:qsz, 1:2], in_=O(1, r)[0:qsz, 128:129]),
                                 reads=[Oslot(1, r), sm], writes=[sm])
                            S.op("dve", lambda e: e.tensor_scalar(out=sm[0:qsz, 1:2], in0=sm[0:qsz, 1:2],
                                                                  scalar1=lamv[0:qsz, 5:6], scalar2=None, op0=ALU.mult),
                                 reads=[sm, lamv], writes=[sm])
                            S.op("dve", lambda e: e.tensor_scalar(out=of[0:qsz, :], in0=O(0, r)[0:qsz, 0:128],
                                                                  scalar1=sm[0:qsz, 0:1], scalar2=None, op0=ALU.mult),
                                 reads=[Oslot(0, r), sm], writes=[of])
                            S.op("dve", lambda e: e.scalar_tensor_tensor(out=of[0:qsz, :], in0=O(1, r)[0:qsz, 0:128],
                                                                         scalar=sm[0:qsz, 1:2], in1=of[0:qsz, :],
                                                                         op0=ALU.mult, op1=ALU.add),
                                 reads=[Oslot(1, r), sm, of], writes=[of])
                            S.op("act", lambda e: e.activation(out=sjunk[0:qsz, :], in_=of[0:qsz, :], func=AF.Square,
                                                               accum_out=sm[0:qsz, 2:3]), reads=[of, sm], writes=[sjunk, sm])
                            S.op("dve", lambda e: e.tensor_scalar(out=sm[0:qsz, 3:4], in0=sm[0:qsz, 2:3], scalar1=1.0 / 128,
                                                                  scalar2=NORM_EPS, op0=ALU.mult, op1=ALU.add),
                                 reads=[sm], writes=[sm])
                            S.op("act", lambda e: e.activation(out=sm[0:qsz, 3:4], in_=sm[0:qsz, 3:4], func=AF.Sqrt),
                                 reads=[sm], writes=[sm])
                            S.op("dve", lambda e: e.reciprocal(out=sm[0:qsz, 3:4], in_=sm[0:qsz, 3:4]), reads=[sm], writes=[sm])
                            S.op("dve", lambda e: e.scalar_tensor_tensor(out=on[0:qsz, :], in0=of[0:qsz, :],
                                                                         scalar=sm[0:qsz, 3:4], in1=gsub[0:qsz, :],
                                                                         op0=ALU.mult, op1=ALU.mult),
                                 reads=[of, sm, gsub], writes=[on])
                        else:
                            S.op("dve", lambda e: e.reciprocal(out=sm[0:qsz, 0:1], in_=O(0, r)[0:qsz, 128:129]),
                                 reads=[Oslot(0, r)], writes=[sm])
                            S.op("dve", lambda e: e.tensor_scalar(out=on[0:qsz, :], in0=O(0, r)[0:qsz, 0:128],
                                                                  scalar1=sm[0:qsz, 0:1], scalar2=None, op0=ALU.mult),
                                 reads=[Oslot(0, r), sm], writes=[on])
                        pst = psS.next()
                        S.op("pe", lambda e: e.transpose(pst[:, 0:qsz], on[0:qsz, :], identf(qsz, qsz)),
                             reads=[on, gc], writes=[pst], inc=True)
                        S.op("act", lambda e: e.activation(out=oT[:, r * 128:r * 128 + qsz], in_=pst[:, 0:qsz], func=AF.Copy),
                             reads=[pst], writes=[oT])
                    S.dma("sp", O_dst, O_dst[h * 128:(h + 1) * 128, qp0:qp0 + nq], oT, oT[:, 0:nq])
            S.end_scope()

        attention("diff")
        if stop == "Bd":
            S.barrier()
            return nc, S
        attention("mla")
        if stop == "Bm":
            S.barrier()
            return nc, S

        S.begin_scope()
        h_sb = S.sbuf("h", [128, 16, n], F32)
        XH = S.sbuf("XH", [128, 16, n], BF16)
        gtrot = Rot(S.sbuf("gt", [128, n], F32, n=4))
        mixed = S.sbuf("mixed", [128, 16, n], F32)
        act = S.sbuf("act", [128, 44, n], BF16)
        tfrot = Rot(S.sbuf("tf", [128, n], F32, n=3))
        sqrot = Rot(S.sbuf("sq", [128, n], BF16, n=3))
        rstd = S.sbuf("rstd", [128, n], F32)
        psr = Rot(PS[0:3])
        PSD = PS[3:7]
        ps_s1 = PS[7]
        jobs = []
        for t in range(NT):
            p0 = t * n

            def j_load(ws, wv, p0=p0):
                S.dma("sp", h_sb, h_sb[:, :, :], h_in, h_in[:, p0:p0 + n].rearrange("(c p) t -> p c t", p=128))
            jobs.append((None, j_load))
            for bi, (XT, wd) in enumerate(((csT, w_brc_d), (oTd, w_brd_d), (oTm, w_brm_d))):
                def j_x(ws, wv, XT=XT, p0=p0):
                    S.dma("sp", XH, XH[:, 0:8, :], XT, XT[:, p0:p0 + n].rearrange("(c p) t -> p c t", p=128))
                jobs.append((None, j_x))
                for g in range(4):
                    def j_br(ws, wv, g=g, bi=bi, p0=p0):
                        gts = []
                        for m in range(4):
                            gm = g * 4 + m
                            gt = gtrot.next()
                            S.dma("sp", gt, gt[:, :], G, G[bi * D_MODEL + gm * 128:bi * D_MODEL + (gm + 1) * 128, p0:p0 + n])
                            gts.append(gt)
                        for m in range(4):
                            gm = g * 4 + m
                            gt = gts[m]
                            ps = psr.next()
                            fm_mm(wv, ws, 8, m, 128, XH, lambda c: XH[:, c, :], ps, n)
                            if bi == 0:
                                S.op("dve", lambda e: e.tensor_tensor(out=mixed[:, gm, :], in0=ps[:, 0:n], in1=gt[:, :], op=ALU.mult),
                                     reads=[ps, gt], writes=[mixed])
                            else:
                                tf = tfrot.next()
                                S.op("dve", lambda e: e.tensor_tensor(out=tf[:, :], in0=ps[:, 0:n], in1=gt[:, :], op=ALU.mult),
                                     reads=[ps, gt], writes=[tf])
                                if bi == 1:
                                    S.op("pool", lambda e: e.tensor_tensor(out=mixed[:, gm, :], in0=mixed[:, gm, :], in1=tf[:, :], op=ALU.add),
                                         reads=[mixed, tf], writes=[mixed])
                                else:
                                    S.op("pool", lambda e: e.tensor_tensor(out=act[:, gm, :], in0=mixed[:, gm, :], in1=tf[:, :], op=ALU.add),
                                         reads=[mixed, tf], writes=[act])
                    jobs.append((wspec(wd, l, 0, 8, g * 512, 512), j_br))
            for g in range(4):
                def j_out(ws, wv, g=g):
                    for m in range(4):
                        gm = g * 4 + m
                        ps = psr.next()
                        fm_mm(wv, ws, 16, m, 128, act, lambda c: act[:, c, :], ps, n)
                        S.op("dve", lambda e: e.tensor_tensor(out=h_sb[:, gm, :], in0=h_sb[:, gm, :], in1=ps[:, 0:n], op=ALU.add),
                             reads=[h_sb, ps], writes=[h_sb])
                jobs.append((wspec(w_out_d, l, 0, 16, g * 512, 512), j_out))

            def j_fnorm(ws, wv):
                rmsnorm_fm(h_sb, lambda c: h_sb[:, c, :], 16, lambda c: lc[:, LC_GFFN + c:LC_GFFN + c + 1], D_MODEL,
                           XH, lambda c: XH[:, c, :], sqrot, rstd, ps_s1, n)
            jobs.append((None, j_fnorm))
            xh_rhs = lambda c: XH[:, c, :]
            for fg in range(11):
                def j_gate(ws, wv, fg=fg):
                    for m in range(4):
                        ps = psr.next()
                        fm_mm(wv, ws, 16, m, 128, XH, xh_rhs, ps, n)
                        S.op("act", lambda e: e.activation(out=mixed[:, m, :], in_=ps[:, 0:n], func=AF.Silu),
                             reads=[ps], writes=[mixed])
                jobs.append((wspec(w_fg_d, l, 0, 16, fg * 512, 512), j_gate))

                def j_up(ws, wv, fg=fg):
                    for m in range(4):
                        ps = psr.next()
                        fm_mm(wv, ws, 16, m, 128, XH, xh_rhs, ps, n)
                        S.op("dve", lambda e: e.tensor_tensor(out=act[:, fg * 4 + m, :], in0=ps[:, 0:n], in1=mixed[:, m, :], op=ALU.mult),
                             reads=[ps, mixed], writes=[act])
                jobs.append((wspec(w_fu_d, l, 0, 16, fg * 512, 512), j_up))
            for cg in range(4):
                fgs = [(0, 16), (16, 16), (32, 12)]
                for fi, (f0, kc) in enumerate(fgs):
                    def j_down(ws, wv, cg=cg, fi=fi, f0=f0, kc=kc, p0=p0):
                        for m in range(4):
                            fm_mm(wv, ws, kc, m, 128, act, lambda c: act[:, f0 + c, :], PSD[m], n,
                                  start=(fi == 0), stop=(fi == 2))
                        if fi == 2:
                            for m in range(4):
                                gm = cg * 4 + m
                                S.op("dve", lambda e: e.tensor_tensor(out=h_sb[:, gm, :], in0=h_sb[:, gm, :], in1=PSD[m][:, 0:n], op=ALU.add),
                                     reads=[h_sb, PSD[m]], writes=[h_sb])
                            if cg == 3:
                                S.dma("sp", h_out, h_out[:, p0:p0 + n].rearrange("(c p) t -> p c t", p=128), h_sb, h_sb[:, :, :])
                    jobs.append((wspec(w_fd_d, l, f0 * 128, kc, cg * 512, 512), j_down))
        run_jobs(jobs)
        S.end_scope()

    S.begin_scope()
    h_fin = hT[DEPTH % 2]
    h_sb = S.sbuf("h", [128, 16, n], F32)
    hnf = S.sbuf("hnf", [128, 16, n], F32)
    sqrot = Rot(S.sbuf("sq", [128, n], BF16, n=3))
    rstd = S.sbuf("rstd", [128, n], F32)
    strot = Rot(S.sbuf("ostg", [128, D_MODEL], F32, n=2))
    psr = Rot(PS[0:6])
    for t in range(NT):
        p0 = t * n
        S.dma("sp", h_sb, h_sb[:, :, :], h_fin, h_fin[:, p0:p0 + n].rearrange("(c p) t -> p c t", p=128))
        rmsnorm_fm(h_sb, lambda c: h_sb[:, c, :], 16, lambda c: gc[:, GC_FIN + c:GC_FIN + c + 1], D_MODEL,
                   hnf, lambda c: hnf[:, c, :], sqrot, rstd, PS[7], n)
        for (s0, sz) in subblocks(n):
            st = strot.next()
            for g4 in range(4):
                ps = psr.next()
                for q in range(4):
                    c = g4 * 4 + q
                    S.op("pe", lambda e, c=c, q=q: e.transpose(ps[0:sz, q * 128:(q + 1) * 128], hnf[:, c, s0:s0 + sz],
                                                               identf(128, 128)),
                         reads=[hnf, gc], writes=[ps], inc=(q == 3))
                if g4 % 2 == 0:
                    S.op("dve", lambda e: e.tensor_copy(st[0:sz, g4 * 512:(g4 + 1) * 512], ps[0:sz, :]), reads=[ps], writes=[st])
                else:
                    S.op("act", lambda e: e.activation(out=st[0:sz, g4 * 512:(g4 + 1) * 512], in_=ps[0:sz, :], func=AF.Copy),
                         reads=[ps], writes=[st])
            pa = p0 + s0
            lo = max(pa, N_META)
            if lo < pa + sz:
                S.dma("sp", out_d, out_d[lo - N_META:pa + sz - N_META, :], st, st[lo - pa:sz, :])
    S.end_scope()
    S.barrier()
    return nc, S


def _t5_bucket(rel):
    nb = 16
    ret = np.where(rel > 0, nb, 0)
    nn = np.abs(rel)
    max_exact = 8
    nf = np.maximum(nn, 1).astype(np.float32)
    large = max_exact + (np.log(nf / max_exact) / math.log(128 / max_exact) * (nb - max_exact)).astype(np.int32)
    large = np.minimum(large, nb - 1)
    return ret + np.where(nn < max_exact, nn, large)


def _chunk_id(pos):
    return np.where(pos < N_META, 0, (pos - N_META) // 64 + 1)


def host_consts(inp, SEQ, DEPTH):
    L = SEQ + N_META
    f32 = np.float32
    perm = np.concatenate([np.arange(32, 64), np.arange(0, 32)])
    w_in = np.ascontiguousarray(inp["w_in"], dtype=f32)
    w_crs = np.ascontiguousarray(w_in[:, :, OFF_CR + perm])
    w_uq = inp["w_uq"].reshape(DEPTH, 512, 8, 192)
    w_uqn = np.ascontiguousarray(w_uq[:, :, :, 0:128].reshape(DEPTH, 512, 1024))
    rope_n = w_uq[:, :, :, 128:192]
    rope_s = rope_n[:, :, :, perm]
    w_uqr = np.ascontiguousarray(np.concatenate([rope_n.reshape(DEPTH, 512, 512), rope_s.reshape(DEPTH, 512, 512)], axis=2))
    w_ukv = inp["w_ukv"].reshape(DEPTH, 256, 8, 256)
    w_kn = np.ascontiguousarray(w_ukv[:, :, :, 0:128].reshape(DEPTH, 256, 1024))
    w_kv = np.ascontiguousarray(w_ukv[:, :, :, 128:256].reshape(DEPTH, 256, 1024))
    lc = np.zeros((DEPTH, 128, NLC), f32)
    fm = lambda v, nch: v.reshape(nch, 128).T
    for l in range(DEPTH):
        lc[l, :, LC_GMIX:LC_GMIX + 16] = fm(inp["norm_mix"][l], 16)
        lc[l, :, LC_GFFN:LC_GFFN + 16] = fm(inp["norm_ffn"][l], 16)
        cw = inp["conv_w"][l]
        lc[l, :, LC_CW:LC_CW + 248] = cw.T.reshape(8, 128, 31).transpose(1, 0, 2).reshape(128, 248)
        lc[l, :, LC_CB:LC_CB + 8] = fm(inp["conv_b"][l], 8)
        lc[l, :, LC_LNG:LC_LNG + 8] = fm(inp["conv_ln_g"][l], 8)
        lc[l, :, LC_LNB:LC_LNB + 8] = fm(inp["conv_ln_b"][l], 8)
        lc[l, :, LC_QNG:LC_QNG + 4] = fm(inp["mla_q_norm"][l], 4)
        lc[l, :, LC_KVG:LC_KVG + 2] = fm(inp["mla_kv_norm"][l], 2)
        lc[l, :, LC_SUB:LC_SUB + 128] = inp["diff_subln"][l][None, :]
        lc[l, :, LC_LQ1:LC_LQ1 + 64] = inp["diff_lq1"][l][None, :]
        lc[l, :, LC_LK1:LC_LK1 + 64] = inp["diff_lk1"][l][None, :]
        lc[l, :, LC_LQ2:LC_LQ2 + 64] = inp["diff_lq2"][l][None, :]
        lc[l, :, LC_LK2:LC_LK2 + 64] = inp["diff_lk2"][l][None, :]
    rel = np.asarray(inp["rel_bias"], f32)
    gcm = np.zeros((128, NGC), f32)
    gcm[:, GC_FIN:GC_FIN + 16] = fm(np.asarray(inp["final_norm"], f32), 16)
    gcm[:, GC_B15:GC_B15 + 8] = rel[15][None, :]
    gcm[:, GC_ID:GC_ID + 128] = np.eye(128, dtype=f32)
    def tile(kpos, qpos):
        relm = kpos[:, None] - qpos[None, :]
        vis = _chunk_id(kpos)[:, None] <= _chunk_id(qpos)[None, :]
        b = rel[_t5_bucket(relm)]
        return np.where(vis[:, :, None], b, f32(NEG)), vis
    ar = np.arange(128)
    biasT = np.zeros((128, 8, 4, 128), f32)
    d, vis_d = tile(N_META + 128 + ar, N_META + 128 + ar)
    biasT[:, :, 0, :] = d.transpose(0, 2, 1)
    p, _ = tile(N_META + ar, N_META + 128 + ar)
    biasT[:, :, 1, :] = p.transpose(0, 2, 1)
    m0, _ = tile(np.arange(N_META), N_META + ar)
    biasT[0:N_META, :, 2, :] = m0.transpose(0, 2, 1)
    mm, _ = tile(np.arange(N_META), np.arange(N_META))
    biasT[0:N_META, :, 3, 0:N_META] = mm.transpose(0, 2, 1)
    gcm[:, GC_MASK:GC_MASK + 128] = np.where(vis_d, f32(0.0), f32(NEG))
    half = 32
    inv = (10000.0 ** (-np.arange(half, dtype=f32) / half)).astype(f32)
    ang = np.arange(L, dtype=f32)[None, :] * inv[:, None]
    cos, sin = np.cos(ang).astype(f32), np.sin(ang).astype(f32)
    C64 = np.concatenate([cos, cos], 0)
    S64 = np.concatenate([-sin, sin], 0)
    ropeC = np.ascontiguousarray(np.concatenate([C64, C64], 0))
    ropeS = np.ascontiguousarray(np.concatenate([S64, S64], 0))
    shared = dict(
        meta=np.ascontiguousarray(inp["meta_tokens"], dtype=f32), w_in=w_in, w_crs=w_crs, w_uqn=w_uqn, w_uqr=w_uqr,
        w_kn=w_kn, w_kv=w_kv, w_br_conv=np.ascontiguousarray(inp["w_br_conv"], dtype=f32),
        w_br_diff=np.ascontiguousarray(inp["w_br_diff"], dtype=f32), w_br_mla=np.ascontiguousarray(inp["w_br_mla"], dtype=f32),
        w_out=np.ascontiguousarray(inp["w_out"], dtype=f32), w_ffn_gate=np.ascontiguousarray(inp["w_ffn_gate"], dtype=f32),
        w_ffn_up=np.ascontiguousarray(inp["w_ffn_up"], dtype=f32), w_ffn_down=np.ascontiguousarray(inp["w_ffn_down"], dtype=f32),
        lc=lc, gc=gcm, biasT=np.ascontiguousarray(biasT.reshape(128, 8 * 4 * 128)), ropeC=ropeC, ropeS=ropeS)
    return shared


def run(inp, TT=456, debug=False, trace=False, stop=None, ncores=4):
    inp = {k: np.asarray(v) for k, v in inp.items()}
    x = inp["x"]
    B, SEQ, _ = x.shape
    DEPTH = inp["w_in"].shape[0]
    nc, S = build(SEQ, DEPTH, TT, debug=debug, stop=stop)
    print('built: ninstr', S.ninstr, 'nsems', len(S.owners) + 5, flush=True)
    shared = host_consts(inp, SEQ, DEPTH)
    in_maps = []
    for c in range(ncores):
        m = dict(shared)
        m["x"] = np.ascontiguousarray(x[c % B], dtype=np.float32)
        in_maps.append(m)
    res = run_bass_kernel_spmd(nc, in_maps, core_ids=list(range(ncores)), **({"trace": True} if trace else {}))
    out = np.stack([res.results[b % ncores]["out"] for b in range(B)], axis=0).astype(np.float32)
    return out, res


def kernel(**inputs):
    out, _ = run(inputs)
    return out
```

```python
import contextlib
import math
import numpy as np
import concourse.bass as bass
import concourse.mybir as mybir
from concourse.bass_utils import run_bass_kernel_spmd

F32 = mybir.dt.float32
BF16 = mybir.dt.bfloat16
AF = mybir.ActivationFunctionType
ALU = mybir.AluOpType
AX = mybir.AxisListType

D_MODEL = 2048
N_META = 16
NORM_EPS = 1e-6
CONV_W = 1024
CONV_K = 31
OFF_A = 0
OFF_BQ = 2048
OFF_BK = 3072
OFF_BV = 4096
OFF_CQ = 5120
OFF_CKV = 5632
OFF_CR = 5888
OFF_G = 5952
IN_COLS = OFF_G + 3 * D_MODEL
D_FF = 5632
NEG = -30000.0
MLA_SCALE = 192.0 ** -0.5

LC_GMIX, LC_GFFN, LC_CW, LC_CB, LC_LNG, LC_LNB, LC_QNG, LC_KVG, LC_SUB, LC_LQ1, LC_LK1, LC_LQ2, LC_LK2 = \
    0, 16, 32, 280, 288, 296, 304, 308, 310, 438, 502, 566, 630
LC_SUBC = 694
NLC = 695
GC_FIN, GC_B15, GC_ID, GC_MASK = 0, 16, 24, 152
NGC = 280

SAME_ENG_SYNC = True


class Eng:
    def __init__(self, key, obj, sem):
        self.key, self.obj, self.sem = key, obj, sem
        self.count = 0
        self.waited = {}


class SemOwner:
    def __init__(self, key, sem):
        self.key, self.sem = key, sem
        self.count = 0


class Slot:
    def __init__(self, t, name, space):
        self.t, self.name, self.space = t, name, space
        self.w = {}
        self.r = {}
        self.dsem = None

    def __getitem__(self, idx):
        return self.t[idx]


class Sched:
    def __init__(self, nc):
        self.nc = nc
        self.es = contextlib.ExitStack()
        self.eng = {}
        for key, obj in (("pe", nc.tensor), ("act", nc.scalar), ("dve", nc.vector),
                         ("pool", nc.gpsimd), ("sp", nc.sync)):
            sem = self.es.enter_context(nc.semaphore("sem_" + key))
            self.eng[key] = Eng(key, obj, sem)
        self.slots = []
        self.owners = []
        self.free_owners = []
        self.ninstr = 0
        self.scope = None
        self.scope_slots = []
        self.uid = 0

    def begin_scope(self):
        self.scope = contextlib.ExitStack()
        self.scope_slots = []

    def end_scope(self):
        self.barrier()
        self.scope.close()
        self.scope = None
        for s in self.scope_slots:
            self.slots.remove(s)
            if s.dsem is not None:
                self.free_owners.append(s.dsem)
                s.dsem = None
        self.scope_slots = []

    def _ctx(self):
        return self.scope if self.scope is not None else self.es

    def sbuf(self, name, shape, dtype, n=1):
        out = []
        for i in range(n):
            self.uid += 1
            nm = f"{name}_{i}_{self.uid}"
            t = self._ctx().enter_context(self.nc.sbuf_tensor(nm, list(shape), dtype))
            s = Slot(t, nm, "sbuf")
            self.slots.append(s)
            if self.scope is not None:
                self.scope_slots.append(s)
            out.append(s)
        return out if n > 1 else out[0]

    def psum(self, name, shape, dtype=F32):
        t = self.es.enter_context(self.nc.psum_tensor(name, list(shape), dtype))
        s = Slot(t, name, "psum")
        self.slots.append(s)
        return s

    def dram(self, name, shape, dtype, kind="Internal"):
        t = self.nc.dram_tensor(name, list(shape), dtype, kind=kind)
        s = Slot(t.ap(), name, "dram")
        self.slots.append(s)
        return s

    def _wait(self, E, dep):
        owner, val = dep
        if owner is E and (not SAME_ENG_SYNC or E.key == "pe"):
            return
        if E.waited.get(owner.key, 0) >= val:
            return
        E.obj.wait_ge(owner.sem, val)
        E.waited[owner.key] = val

    def _deps(self, E, reads, writes):
        for s in reads:
            for d in s.w.values():
                self._wait(E, d)
        for s in writes:
            for d in s.w.values():
                self._wait(E, d)
            for d in s.r.values():
                self._wait(E, d)

    def op(self, eng, fn, reads=(), writes=(), inc=True):
        E = self.eng[eng]
        self._deps(E, reads, writes)
        ins = fn(E.obj)
        self.ninstr += 1
        tgt = E.count + 1
        if inc:
            ins.then_inc(E.sem, 1)
            E.count = tgt
        for s in writes:
            s.w = {E.key: (E, tgt)}
            s.r = {}
        for s in reads:
            if s not in writes:
                s.r[E.key] = (E, tgt)
        return ins

    def dma(self, q, out_slot, out_ap, in_slot, in_ap):
        E = self.eng[q]
        self._deps(E, [in_slot], [out_slot])
        sl = out_slot if out_slot.space == "sbuf" else in_slot
        assert sl.space == "sbuf"
        if sl.dsem is None:
            if self.free_owners:
                sl.dsem = self.free_owners.pop()
            else:
                sem = self.es.enter_context(self.nc.semaphore(f"dsem{len(self.owners)}"))
                sl.dsem = SemOwner(f"d{len(self.owners)}", sem)
                self.owners.append(sl.dsem)
        ow = sl.dsem
        ins = E.obj.dma_start(out=out_ap, in_=in_ap)
        self.ninstr += 1
        ow.count += 16
        ins.then_inc(ow.sem, 16)
        dep = (ow, ow.count)
        if out_slot.space == "dram":
            out_slot.w[ow.key] = dep
        else:
            out_slot.w = {ow.key: dep}
            out_slot.r = {}
        in_slot.r[ow.key] = dep
        return ins

    def barrier(self):
        deps = [(E, E.count) for E in self.eng.values() if E.count > 0]
        deps += [(o, o.count) for o in self.owners if o.count > 0]
        for E in self.eng.values():
            for d in deps:
                self._wait(E, d)
        for s in self.slots:
            s.w = {}
            s.r = {}


class Rot:
    def __init__(self, items):
        self.items = list(items)
        self.i = 0

    def next(self):
        s = self.items[self.i % len(self.items)]
        self.i += 1
        return s


def subblocks(n, b=128):
    return [(s, min(b, n - s)) for s in range(0, n, b)]


def build(SEQ, DEPTH, TT, debug=False, stop=None):
    L = SEQ + N_META
    assert L % TT == 0 and SEQ % 128 == 0
    NT = L // TT
    NB = SEQ // 128
    n = TT
    nc = bass.Bass("TRN2", target_bir_lowering=False)
    S = Sched(nc)

    def ext(name, shape, dtype=F32):
        return Slot(nc.dram_tensor(name, list(shape), dtype, kind="ExternalInput").ap(), name, "dram")

    x_d = ext("x", [SEQ, D_MODEL])
    meta_d = ext("meta", [N_META, D_MODEL])
    w_in_d = ext("w_in", [DEPTH, D_MODEL, IN_COLS])
    w_crs_d = ext("w_crs", [DEPTH, D_MODEL, 64])
    w_uqn_d = ext("w_uqn", [DEPTH, 512, 1024])
    w_uqr_d = ext("w_uqr", [DEPTH, 512, 1024])
    w_kn_d = ext("w_kn", [DEPTH, 256, 1024])
    w_kv_d = ext("w_kv", [DEPTH, 256, 1024])
    w_brc_d = ext("w_br_conv", [DEPTH, 1024, D_MODEL])
    w_brd_d = ext("w_br_diff", [DEPTH, 1024, D_MODEL])
    w_brm_d = ext("w_br_mla", [DEPTH, 1024, D_MODEL])
    w_out_d = ext("w_out", [DEPTH, D_MODEL, D_MODEL])
    w_fg_d = ext("w_ffn_gate", [DEPTH, D_MODEL, D_FF])
    w_fu_d = ext("w_ffn_up", [DEPTH, D_MODEL, D_FF])
    w_fd_d = ext("w_ffn_down", [DEPTH, D_FF, D_MODEL])
    lc_d = ext("lc", [DEPTH, 128, NLC])
    gc_d = ext("gc", [128, NGC])
    bias_d = ext("biasT", [128, 8 * 4 * 128])
    ropeC_d = ext("ropeC", [128, L])
    ropeS_d = ext("ropeS", [128, L])
    out_t = nc.dram_tensor("out", [SEQ, D_MODEL], F32, kind="ExternalOutput")
    out_d = Slot(out_t.ap(), "out", "dram")

    dk = "ExternalOutput" if debug else "Internal"
    hT = [S.dram(f"hT{i}", [D_MODEL, L], F32, kind=dk) for i in range(2)]
    csT = S.dram("csT", [1024, L], BF16, kind=dk)
    QdT = S.dram("QdT", [1024, L], BF16, kind=dk)
    KdT = S.dram("KdT", [1024, L], BF16, kind=dk)
    Vd = S.dram("Vd", [L, 1024], BF16, kind=dk)
    QnT = S.dram("QnT", [1024, L], BF16, kind=dk)
    QrT = S.dram("QrT", [512, L], BF16, kind=dk)
    KnT = S.dram("KnT", [1024, L], BF16, kind=dk)
    KrT = S.dram("KrT", [64, L], BF16, kind=dk)
    Vm = S.dram("Vm", [L, 1024], BF16, kind=dk)
    G = S.dram("G", [3 * D_MODEL, L], F32, kind=dk)
    oTd = S.dram("oTd", [1024, L], BF16, kind=dk)
    oTm = S.dram("oTm", [1024, L], BF16, kind=dk)

    gc = S.sbuf("gc", [128, NGC], F32)
    lc = S.sbuf("lc", [128, NLC], F32)
    ones_bf = S.sbuf("ones_bf", [128, 128], BF16)
    ones_f = S.sbuf("ones_f", [128, 128], F32)
    lamv = S.sbuf("lamv", [128, 8], F32)
    gsub = S.sbuf("gsub", [128, 128], F32)
    junk = S.sbuf("junk", [128, 128], F32)
    WSLOTS = S.sbuf("w", [128, 16 * 512], BF16, n=3)
    wrot = Rot(WSLOTS)
    PS = [S.psum(f"ps{i}", [128, 512]) for i in range(8)]

    identf = lambda a, b: gc[0:a, GC_ID:GC_ID + b]


    S.dma("sp", gc, gc[:, :], gc_d, gc_d[:, :])
    S.op("pool", lambda e: e.memset(ones_bf[:, :], 1.0), writes=[ones_bf])
    S.op("pool", lambda e: e.memset(ones_f[:, :], 1.0), writes=[ones_f])

    bg = []

    def drain_bg(k=None):
        cnt = len(bg) if k is None else min(k, len(bg))
        for _ in range(cnt):
            bg.pop(0)()

    def run_jobs(jobs, pref=2, bg_per_job=12):
        loaded = {}

        def ensure(i):
            if i < len(jobs) and jobs[i][0] is not None and i not in loaded:
                src_slot, src_ap, kc, ncols = jobs[i][0]
                slot = wrot.next()
                view = slot[:, 0:kc * ncols].rearrange("p (c n) -> p c n", n=ncols)
                S.dma("pool", slot, view, src_slot, src_ap)
                loaded[i] = (slot, view)

        for i in range(len(jobs)):
            for j in range(i, min(i + pref + 1, len(jobs))):
                ensure(j)
            jobs[i][1](*(loaded.pop(i) if i in loaded else (None, None)))
            drain_bg(bg_per_job)
        drain_bg()

    def wspec(wd, l, r0, kc, c0, ncols):
        ap = wd[l, r0:r0 + kc * 128, c0:c0 + ncols].rearrange("(c p) n -> p c n", p=128)
        return (wd, ap, kc, ncols)

    def rmsnorm_fm(x_slot, xv, nch, gain, Dn, out_slot, ov, sqrot, rstd_slot, psb, nn, pp=128):
        for c in range(nch):
            sq = sqrot.next()
            S.op("act", lambda e, c=c, sq=sq: e.activation(out=sq[0:pp, 0:nn], in_=xv(c), func=AF.Square),
                 reads=[x_slot], writes=[sq])
            S.op("pe", lambda e, c=c, sq=sq: e.matmul(psb[:, 0:nn], ones_bf[0:pp, :], sq[0:pp, 0:nn],
                                                      start=(c == 0), stop=(c == nch - 1)),
                 reads=[sq, ones_bf], writes=[psb], inc=True)
        S.op("dve", lambda e: e.tensor_scalar(out=rstd_slot[:, 0:nn], in0=psb[:, 0:nn], scalar1=1.0 / Dn,
                                              scalar2=NORM_EPS, op0=ALU.mult, op1=ALU.add),
             reads=[psb], writes=[rstd_slot])
        S.op("act", lambda e: e.activation(out=rstd_slot[:, 0:nn], in_=rstd_slot[:, 0:nn], func=AF.Sqrt),
             reads=[rstd_slot], writes=[rstd_slot])
        S.op("dve", lambda e: e.reciprocal(out=rstd_slot[:, 0:nn], in_=rstd_slot[:, 0:nn]),
             reads=[rstd_slot], writes=[rstd_slot])
        for c in range(nch):
            S.op("dve", lambda e, c=c: e.scalar_tensor_tensor(out=ov(c), in0=xv(c), scalar=gain(c),
                                                              in1=rstd_slot[0:pp, 0:nn], op0=ALU.mult, op1=ALU.mult),
                 reads=[x_slot, rstd_slot, lc, gc], writes=[out_slot])

    def fm_mm(view, wslot, kc, m, msz, rhs_slot, rhs, ps, nn, start=True, stop=True):
        for c in range(kc):
            S.op("pe", lambda e, c=c: e.matmul(ps[0:msz, 0:nn], view[:, c, m * 128:m * 128 + msz], rhs(c),
                                               start=(start and c == 0), stop=(stop and c == kc - 1)),
                 reads=[wslot, rhs_slot], writes=[ps], inc=(c == kc - 1))

    S.begin_scope()
    xt = S.sbuf("xt", [128, D_MODEL], F32, n=2)
    xrot = Rot(xt)
    stg = S.sbuf("pstg", [128, 16, 128], F32, n=2)
    srot = Rot(stg)
    psr = Rot(PS[0:4])
    for (p0, rows) in subblocks(L):
        t = xrot.next()
        if p0 == 0:
            S.dma("sp", t, t[0:N_META, :], meta_d, meta_d[:, :])
            S.dma("sp", t, t[N_META:rows, :], x_d, x_d[0:rows - N_META, :])
        else:
            S.dma("sp", t, t[0:rows, :], x_d, x_d[p0 - N_META:p0 - N_META + rows, :])
        st = srot.next()
        for g4 in range(4):
            ps = psr.next()
            for q in range(4):
                c = g4 * 4 + q
                S.op("pe", lambda e, c=c, q=q, ps=ps, t=t: e.transpose(ps[:, q * 128:q * 128 + rows],
                                                                     t[0:rows, c * 128:(c + 1) * 128],
                                                                     identf(rows, rows)),
                     reads=[t, gc], writes=[ps], inc=(q == 3))
            S.op("dve" if g4 % 2 == 0 else "act",
                 (lambda e, ps=ps, st=st, g4=g4: e.tensor_copy(
                     st[:, g4 * 4:g4 * 4 + 4, 0:rows],
                     ps[:, :].rearrange("p (q t) -> p q t", t=128)[:, :, 0:rows])) if g4 % 2 == 0 else
                 (lambda e, ps=ps, st=st, g4=g4: e.activation(
                     out=st[:, g4 * 4:g4 * 4 + 4, 0:rows],
                     in_=ps[:, :].rearrange("p (q t) -> p q t", t=128)[:, :, 0:rows], func=AF.Copy)),
                 reads=[ps], writes=[st])
        S.dma("sp", hT[0], hT[0][:, p0:p0 + rows].rearrange("(c p) t -> p c t", p=128), st, st[:, :, 0:rows])
    S.end_scope()
    if stop == "pro":
        S.barrier()
        return nc, S

    for l in range(DEPTH):
        lam_init = 0.8 - 0.6 * math.exp(-0.3 * l)
        h_in, h_out = hT[l % 2], hT[(l + 1) % 2]
        S.dma("sp", lc, lc[:, :], lc_d, lc_d[l, :, :])
        S.op("dve", lambda e: e.tensor_tensor(out=junk[:, 0:64], in0=lc[:, LC_LQ1:LC_LQ1 + 64],
                                              in1=lc[:, LC_LK1:LC_LK1 + 64], op=ALU.mult), reads=[lc], writes=[junk])
        S.op("dve", lambda e: e.tensor_reduce(out=lamv[:, 0:1], in_=junk[:, 0:64], axis=AX.X, op=ALU.add),
             reads=[junk], writes=[lamv])
        S.op("dve", lambda e: e.tensor_tensor(out=junk[:, 0:64], in0=lc[:, LC_LQ2:LC_LQ2 + 64],
                                              in1=lc[:, LC_LK2:LC_LK2 + 64], op=ALU.mult), reads=[lc], writes=[junk])
        S.op("dve", lambda e: e.tensor_reduce(out=lamv[:, 1:2], in_=junk[:, 0:64], axis=AX.X, op=ALU.add),
             reads=[junk, lamv], writes=[lamv])
        S.op("act", lambda e: e.activation(out=lamv[:, 2:4], in_=lamv[:, 0:2], func=AF.Exp), reads=[lamv], writes=[lamv])
        S.op("dve", lambda e: e.tensor_tensor(out=lamv[:, 4:5], in0=lamv[:, 2:3], in1=lamv[:, 3:4], op=ALU.subtract),
             reads=[lamv], writes=[lamv])
        S.op("dve", lambda e: e.tensor_scalar(out=lamv[:, 5:6], in0=lamv[:, 4:5], scalar1=lam_init, scalar2=-1.0,
                                              op0=ALU.add, op1=ALU.mult), reads=[lamv], writes=[lamv])
        S.op("dve", lambda e: e.tensor_scalar(out=lamv[:, 6:7], in0=lc[:, LC_SUBC:LC_SUBC + 1], scalar1=1.0 - lam_init,
                                              scalar2=None, op0=ALU.mult), reads=[lc, lamv], writes=[lamv])

        S.begin_scope()
        h_sb = S.sbuf("h", [128, 16, n], F32)
        sqrot = Rot(S.sbuf("sq", [128, n], BF16, n=3))
        rstd = S.sbuf("rstd", [128, n], F32)
        hn = S.sbuf("hn", [128, 16, n], BF16)
        a_buf = S.sbuf("abuf", [128, 8, 32 + n], F32)
        acc = S.sbuf("acc", [128, 8, n], F32)
        accs = [Slot(acc.t, f"acc_c{c}", "sbuf") for c in range(8)]
        S.slots.extend(accs)
        S.scope_slots.extend(accs)
        tfrot = Rot(S.sbuf("tf", [128, n], F32, n=3))
        cs = S.sbuf("cs", [128, 8, n], BF16)
        cq = S.sbuf("cq", [128, 4, n], F32)
        cqn = S.sbuf("cqn", [128, 4, n], BF16)
        ckv = S.sbuf("ckv", [128, 2, n], F32)
        ckvn = S.sbuf("ckvn", [128, 2, n], BF16)
        kr = S.sbuf("kr", [128, 2, n], F32)
        sbrot = Rot(S.sbuf("sb", [128, n], BF16, n=4))
        sfrot = Rot(S.sbuf("sf", [128, n], F32, n=3))
        svrot = Rot(S.sbuf("sv", [128, 512], BF16, n=3))
        rCs = S.sbuf("rC", [128, n], F32, n=2)
        rSs = S.sbuf("rS", [128, n], F32, n=2)
        psr = Rot(PS[0:6])
        ps_s1, ps_s2 = PS[6], PS[7]
        S.op("pool", lambda e: e.memset(a_buf[:, :, 0:32], 0.0), writes=[a_buf])
        jobs = []
        for t in range(NT):
            p0 = t * n
            tsb = subblocks(n)

            rC, rS = rCs[t % 2], rSs[t % 2]

            def j_norm(ws, wv, p0=p0, t=t):
                def loads(tt):
                    q0 = tt * n
                    S.dma("sp", h_sb, h_sb[:, :, :], h_in, h_in[:, q0:q0 + n].rearrange("(c p) t -> p c t", p=128))
                    S.dma("sp", rCs[tt % 2], rCs[tt % 2][:, :], ropeC_d, ropeC_d[:, q0:q0 + n])
                    S.dma("sp", rSs[tt % 2], rSs[tt % 2][:, :], ropeS_d, ropeS_d[:, q0:q0 + n])
                if t == 0:
                    loads(0)
                rmsnorm_fm(h_sb, lambda c: h_sb[:, c, :], 16, lambda c: lc[:, LC_GMIX + c:LC_GMIX + c + 1], D_MODEL,
                           hn, lambda c: hn[:, c, :], sqrot, rstd, ps_s1, n)
                if t + 1 < NT:
                    loads(t + 1)
            jobs.append((None, j_norm))

            hn_rhs = lambda c: hn[:, c, :]

            def mk_fm(handler, nchunk, chunk_sizes=None):
                def job(ws, wv):
                    for m in range(nchunk):
                        msz = 128 if chunk_sizes is None else chunk_sizes[m]
                        ps = psr.next()
                        fm_mm(wv, ws, 16, m, msz, hn, hn_rhs, ps, n)
                        handler(m, ps)
                return job

            for g in range(2):
                def h_gate(m, ps, g=g):
                    gm = g * 4 + m
                    S.op("act", lambda e: e.activation(out=a_buf[:, gm, 32:32 + n], in_=ps[:, 0:n], func=AF.Sigmoid),
                         reads=[ps], writes=[a_buf])
                jobs.append((wspec(w_in_d, l, 0, 16, OFF_A + CONV_W + g * 512, 512), mk_fm(h_gate, 4)))
            for g in range(2):
                def h_val(m, ps, g=g):
                    gm = g * 4 + m
                    S.op("dve", lambda e: e.tensor_tensor(out=a_buf[:, gm, 32:32 + n], in0=ps[:, 0:n],
                                                          in1=a_buf[:, gm, 32:32 + n], op=ALU.mult),
                         reads=[ps, a_buf], writes=[a_buf])
                jobs.append((wspec(w_in_d, l, 0, 16, OFF_A + g * 512, 512), mk_fm(h_val, 4)))

            def j_conv_taps(ws, wv):
                for j in range(CONV_K):
                    for c in range(8):
                        if j == 0:
                            bg.append(lambda c=c: S.op("dve", lambda e: e.tensor_scalar(
                                out=acc[:, c, :], in0=a_buf[:, c, 2:2 + n],
                                scalar1=lc[:, LC_CW + c * 31:LC_CW + c * 31 + 1], scalar2=lc[:, LC_CB + c:LC_CB + c + 1],
                                op0=ALU.mult, op1=ALU.add), reads=[a_buf, lc], writes=[accs[c], acc]))
                        else:
                            bg.append(lambda c=c, j=j: S.op("dve", lambda e: e.scalar_tensor_tensor(
                                out=acc[:, c, :], in0=a_buf[:, c, 2 + j:2 + j + n],
                                scalar=lc[:, LC_CW + c * 31 + j:LC_CW + c * 31 + j + 1], in1=acc[:, c, :],
                                op0=ALU.mult, op1=ALU.add), reads=[a_buf, lc, accs[c]], writes=[accs[c]]))
            jobs.append((None, j_conv_taps))

            def j_conv(ws, wv, p0=p0):
                drain_bg()
                for c in range(8):
                    for k_, d_ in accs[c].w.items():
                        acc.w[k_] = d_
                    accs[c].w = {}
                    accs[c].r = {}
                S.op("act", lambda e: e.activation(out=a_buf[:, :, 0:32], in_=a_buf[:, :, n:n + 32], func=AF.Copy),
                     reads=[a_buf, acc], writes=[a_buf])
                for c in range(8):
                    tf = tfrot.next()
                    S.op("act", lambda e, c=c, tf=tf: e.activation(out=tf[:, :], in_=acc[:, c, :], func=AF.Square),
                         reads=[acc], writes=[tf])
                    S.op("pe", lambda e, c=c: e.matmul(ps_s1[:, 0:n], ones_f[:, :], acc[:, c, :], start=(c == 0), stop=(c == 7)),
                         reads=[acc, ones_f], writes=[ps_s1], inc=(c == 7))
                    S.op("pe", lambda e, c=c, tf=tf: e.matmul(ps_s2[:, 0:n], ones_f[:, :], tf[:, :], start=(c == 0), stop=(c == 7)),
                         reads=[tf, ones_f], writes=[ps_s2], inc=True)
                mean = tfrot.next()
                S.op("dve", lambda e: e.tensor_scalar(out=mean[:, :], in0=ps_s1[:, 0:n], scalar1=1.0 / CONV_W, scalar2=None,
                                                      op0=ALU.mult), reads=[ps_s1], writes=[mean])
                msq = tfrot.next()
                S.op("dve", lambda e: e.tensor_tensor(out=msq[:, :], in0=mean[:, :], in1=mean[:, :], op=ALU.mult),
                     reads=[mean], writes=[msq])
                S.op("dve", lambda e: e.scalar_tensor_tensor(out=msq[:, :], in0=ps_s2[:, 0:n], scalar=1.0 / CONV_W,
                                                             in1=msq[:, :], op0=ALU.mult, op1=ALU.subtract),
                     reads=[ps_s2, msq], writes=[msq])
                S.op("dve", lambda e: e.tensor_scalar(out=msq[:, :], in0=msq[:, :], scalar1=NORM_EPS, scalar2=None,
                                                      op0=ALU.add), reads=[msq], writes=[msq])
                S.op("act", lambda e: e.activation(out=msq[:, :], in_=msq[:, :], func=AF.Sqrt), reads=[msq], writes=[msq])
                S.op("dve", lambda e: e.reciprocal(out=msq[:, :], in_=msq[:, :]), reads=[msq], writes=[msq])
                for c in range(8):
                    S.op("dve", lambda e, c=c: e.tensor_tensor(out=acc[:, c, :], in0=acc[:, c, :], in1=mean[:, :],
                                                               op=ALU.subtract), reads=[acc, mean], writes=[acc])
                    S.op("dve", lambda e, c=c: e.tensor_tensor(out=acc[:, c, :], in0=acc[:, c, :], in1=msq[:, :],
                                                               op=ALU.mult), reads=[acc, msq], writes=[acc])
                    S.op("act", lambda e, c=c: e.activation(out=cs[:, c, :], in_=acc[:, c, :], func=AF.Silu,
                                                            scale=lc[:, LC_LNG + c:LC_LNG + c + 1],
                                                            bias=lc[:, LC_LNB + c:LC_LNB + c + 1]),
                         reads=[acc, lc], writes=[cs])
                S.dma("sp", csT, csT[:, p0:p0 + n].rearrange("(c p) t -> p c t", p=128), cs, cs[:, :, :])
            for g in range(2):
                def h_bq(m, ps, g=g, p0=p0):
                    gm = g * 4 + m
                    sb = sbrot.next()
                    S.op("act", lambda e: e.activation(out=sb[:, :], in_=ps[:, 0:n], func=AF.Copy, scale=0.125),
                         reads=[ps], writes=[sb])
                    S.dma("sp", QdT, QdT[gm * 128:(gm + 1) * 128, p0:p0 + n], sb, sb[:, :])
                jobs.append((wspec(w_in_d, l, 0, 16, OFF_BQ + g * 512, 512), mk_fm(h_bq, 4)))
            for g in range(2):
                def h_bk(m, ps, g=g, p0=p0):
                    gm = g * 4 + m
                    sb = sbrot.next()
                    S.op("dve", lambda e: e.tensor_copy(sb[:, :], ps[:, 0:n]), reads=[ps], writes=[sb])
                    S.dma("sp", KdT, KdT[gm * 128:(gm + 1) * 128, p0:p0 + n], sb, sb[:, :])
                jobs.append((wspec(w_in_d, l, 0, 16, OFF_BK + g * 512, 512), mk_fm(h_bk, 4)))

            def mk_tm(act_slot, actv, kc, dst, colbase, p0=p0, tsb=tsb):
                def job(ws, wv):
                    ncols = 512
                    for (s0, sz) in tsb:
                        ps = psr.next()
                        for c in range(kc):
                            S.op("pe", lambda e, c=c: e.matmul(ps[0:sz, 0:ncols], actv(c, s0, sz), wv[:, c, 0:ncols],
                                                               start=(c == 0), stop=(c == kc - 1)),
                                 reads=[ws, act_slot], writes=[ps], inc=(c == kc - 1))
                        sv = svrot.next()
                        S.op("dve", lambda e: e.tensor_copy(sv[0:sz, :], ps[0:sz, 0:ncols]), reads=[ps], writes=[sv])
                        S.dma("sp", dst, dst[p0 + s0:p0 + s0 + sz, colbase:colbase + ncols], sv, sv[0:sz, :])
                return job
            for g in range(2):
                jobs.append((wspec(w_in_d, l, 0, 16, OFF_BV + g * 512, 512),
                             mk_tm(hn, lambda c, s0, sz: hn[:, c, s0:s0 + sz], 16, Vd, g * 512)))

            def h_cq(m, ps):
                S.op("dve", lambda e: e.tensor_copy(cq[:, m, :], ps[:, 0:n]), reads=[ps], writes=[cq])
            jobs.append((wspec(w_in_d, l, 0, 16, OFF_CQ, 512), mk_fm(h_cq, 4)))

            def h_ckv(m, ps):
                if m < 2:
                    S.op("dve", lambda e: e.tensor_copy(ckv[:, m, :], ps[:, 0:n]), reads=[ps], writes=[ckv])
                else:
                    S.op("dve", lambda e: e.tensor_copy(kr[0:64, 0, :], ps[0:64, 0:n]), reads=[ps], writes=[kr])
            jobs.append((wspec(w_in_d, l, 0, 16, OFF_CKV, 320), mk_fm(h_ckv, 3, [128, 128, 64])))

            def h_crs(m, ps, p0=p0, rC=rC, rS=rS):
                S.op("dve", lambda e: e.tensor_copy(kr[0:64, 1, :], ps[0:64, 0:n]), reads=[ps], writes=[kr])
                t1 = sfrot.next()
                S.op("dve", lambda e: e.tensor_tensor(out=t1[0:64, :], in0=kr[0:64, 0, :], in1=rC[0:64, :], op=ALU.mult),
                     reads=[kr, rC], writes=[t1])
                t2 = sfrot.next()
                S.op("dve", lambda e: e.tensor_tensor(out=t2[0:64, :], in0=kr[0:64, 1, :], in1=rS[0:64, :], op=ALU.mult),
                     reads=[kr, rS], writes=[t2])
                sb = sbrot.next()
                S.op("dve", lambda e: e.tensor_tensor(out=sb[0:64, :], in0=t1[0:64, :], in1=t2[0:64, :], op=ALU.add),
                     reads=[t1, t2], writes=[sb])
                S.dma("sp", KrT, KrT[:, p0:p0 + n], sb, sb[0:64, :])
            jobs.append((wspec(w_crs_d, l, 0, 16, 0, 64), mk_fm(h_crs, 1, [64])))

            def j_qnorm(ws, wv):
                rmsnorm_fm(cq, lambda c: cq[:, c, :], 4, lambda c: lc[:, LC_QNG + c:LC_QNG + c + 1], 512,
                           cqn, lambda c: cqn[:, c, :], sqrot, rstd, ps_s1, n)
            jobs.append((None, j_qnorm))
            cqn_rhs = lambda c: cqn[:, c, :]
            for g in range(2):
                def j_qn(ws, wv, g=g, p0=p0):
                    for m in range(4):
                        gm = g * 4 + m
                        ps = psr.next()
                        fm_mm(wv, ws, 4, m, 128, cqn, cqn_rhs, ps, n)
                        sb = sbrot.next()
                        S.op("act", lambda e: e.activation(out=sb[:, :], in_=ps[:, 0:n], func=AF.Copy, scale=MLA_SCALE),
                             reads=[ps], writes=[sb])
                        S.dma("sp", QnT, QnT[gm * 128:(gm + 1) * 128, p0:p0 + n], sb, sb[:, :])
                jobs.append((wspec(w_uqn_d, l, 0, 4, g * 512, 512), j_qn))

            def j_qr(ws, wv, p0=p0, rC=rC, rS=rS):
                for m in range(4):
                    psa = psr.next()
                    fm_mm(wv, ws, 4, m, 128, cqn, cqn_rhs, psa, n)
                    psb = psr.next()
                    fm_mm(wv, ws, 4, 4 + m, 128, cqn, cqn_rhs, psb, n)
                    t1 = sfrot.next()
                    S.op("dve", lambda e: e.scalar_tensor_tensor(out=t1[:, :], in0=psa[:, 0:n], scalar=MLA_SCALE,
                                                                 in1=rC[:, :], op0=ALU.mult, op1=ALU.mult),
                         reads=[psa, rC], writes=[t1])
                    t2 = sfrot.next()
                    S.op("dve", lambda e: e.scalar_tensor_tensor(out=t2[:, :], in0=psb[:, 0:n], scalar=MLA_SCALE,
                                                                 in1=rS[:, :], op0=ALU.mult, op1=ALU.mult),
                         reads=[psb, rS], writes=[t2])
                    sb = sbrot.next()
                    S.op("dve", lambda e: e.tensor_tensor(out=sb[:, :], in0=t1[:, :], in1=t2[:, :], op=ALU.add),
                         reads=[t1, t2], writes=[sb])
                    S.dma("sp", QrT, QrT[m * 128:(m + 1) * 128, p0:p0 + n], sb, sb[:, :])
            jobs.append((wspec(w_uqr_d, l, 0, 4, 0, 1024), j_qr))

            def j_kvnorm(ws, wv):
                rmsnorm_fm(ckv, lambda c: ckv[:, c, :], 2, lambda c: lc[:, LC_KVG + c:LC_KVG + c + 1], 256,
                           ckvn, lambda c: ckvn[:, c, :], sqrot, rstd, ps_s1, n)
            jobs.append((None, j_kvnorm))
            ckvn_rhs = lambda c: ckvn[:, c, :]

            def j_kn(ws, wv, p0=p0):
                for m in range(8):
                    ps = psr.next()
                    fm_mm(wv, ws, 2, m, 128, ckvn, ckvn_rhs, ps, n)
                    sb = sbrot.next()
                    S.op("dve" if m % 2 else "act",
                         (lambda e: e.tensor_copy(sb[:, :], ps[:, 0:n])) if m % 2 else
                         (lambda e: e.activation(out=sb[:, :], in_=ps[:, 0:n], func=AF.Copy)),
                         reads=[ps], writes=[sb])
                    S.dma("sp", KnT, KnT[m * 128:(m + 1) * 128, p0:p0 + n], sb, sb[:, :])
            jobs.append((wspec(w_kn_d, l, 0, 2, 0, 1024), j_kn))

            def j_kvv(ws, wv, p0=p0, tsb=tsb):
                for (s0, sz) in tsb:
                    for half in range(2):
                        ps = psr.next()
                        for c in range(2):
                            S.op("pe", lambda e, c=c: e.matmul(ps[0:sz, 0:512], ckvn[:, c, s0:s0 + sz],
                                                               wv[:, c, half * 512:(half + 1) * 512],
                                                               start=(c == 0), stop=(c == 1)),
                                 reads=[ws, ckvn], writes=[ps], inc=(c == 1))
                        sv = svrot.next()
                        S.op("dve", lambda e: e.tensor_copy(sv[0:sz, :], ps[0:sz, 0:512]), reads=[ps], writes=[sv])
                        S.dma("sp", Vm, Vm[p0 + s0:p0 + s0 + sz, half * 512:(half + 1) * 512], sv, sv[0:sz, :])
            jobs.append((wspec(w_kv_d, l, 0, 2, 0, 1024), j_kvv))

            for g in range(12):
                def h_g(m, ps, g=g, p0=p0):
                    gm = g * 4 + m
                    sf = sfrot.next()
                    S.op("act", lambda e: e.activation(out=sf[:, :], in_=ps[:, 0:n], func=AF.Sigmoid),
                         reads=[ps], writes=[sf])
                    S.dma("sp", G, G[gm * 128:(gm + 1) * 128, p0:p0 + n], sf, sf[:, :])
                jobs.append((wspec(w_in_d, l, 0, 16, OFF_G + g * 512, 512), mk_fm(h_g, 4)))
            jobs.append((None, j_conv))
        run_jobs(jobs)
        S.end_scope()
        if stop == "A":
            S.barrier()
            return nc, S

        def attention(kind):
            S.begin_scope()
            diff = (kind == "diff")
            nsm = 2 if diff else 1
            Kt = S.sbuf("Kt", [128, L], BF16, n=2)
            Vs = S.sbuf("Vs", [128, NB + 1, 128], BF16, n=2)
            qrot = Rot(S.sbuf("Qt", [128, 512], BF16, n=2))
            if not diff:
                Krt = S.sbuf("Krt", [64, L], BF16)
                S.dma("sp", Krt, Krt[:, :], KrT, KrT[:, :])
                qrrot = Rot(S.sbuf("Qr", [64, 512], BF16, n=2))
            Pt = [Rot(S.sbuf(f"Pt{s_}", [128, 512], BF16, n=3)) for s_ in range(nsm)]
            rrot = Rot(S.sbuf("rr", [128, 512], F32, n=3))
            trot = Rot(S.sbuf("tt", [128, 512], F32, n=3))
            oTrot = Rot(S.sbuf("oT", [128, 512], BF16, n=2))
            psS = Rot(PS[0:4])
            if diff:
                orot = Rot([[(PS[4], PS[5]), (PS[6], PS[7])]])
            else:
                orot = Rot([[(PS[4], PS[5])], [(PS[6], PS[7])]])
            QT_src = QdT if diff else QnT
            KT_src = KdT if diff else KnT
            V_src = Vd if diff else Vm
            O_dst = oTd if diff else oTm
            qtiles = [(0, [N_META], True)]
            for i in range((NB + 3) // 4):
                nsub = min(4, NB - 4 * i)
                qtiles.append((N_META + 512 * i, [128] * nsub, False))
            bias2 = S.sbuf("bias", [128, 512], F32, n=2)
            for h in range(8):
                K = Kt[h % 2]
                V = Vs[h % 2]
                bias_sb = bias2[h % 2]
                if diff:
                    S.dma("sp", bias_sb, bias_sb[:, :], bias_d, bias_d[:, h * 512:(h + 1) * 512])
                S.dma("sp", K, K[:, :], KT_src, KT_src[h * 128:(h + 1) * 128, :])
                S.dma("sp", V, V[0:N_META, 0, :], V_src, V_src[0:N_META, h * 128:(h + 1) * 128])
                S.dma("sp", V, V[:, 1:NB + 1, :], V_src,
                      V_src[N_META:L, h * 128:(h + 1) * 128].rearrange("(j p) d -> p j d", p=128))
                for ti, (qp0, sizes, is_meta) in enumerate(qtiles):
                    nq = sum(sizes)
                    nsub = len(sizes)
                    qsz = sizes[0]
                    i = ti - 1
                    Q = qrot.next()
                    S.dma("sp", Q, Q[:, 0:nq], QT_src, QT_src[h * 128:(h + 1) * 128, qp0:qp0 + nq])
                    if not diff:
                        Qrs = qrrot.next()
                        S.dma("sp", Qrs, Qrs[:, 0:nq], QrT, QrT[h * 64:(h + 1) * 64, qp0:qp0 + nq])
                    Ob = orot.next()
                    kbs = [(0, N_META, 0, -1)]
                    if not is_meta:
                        for j in range(4 * i + nsub):
                            kbs.append((N_META + 128 * j, 128, j + 1, j))

                    def do_scores(kc0, nk, vblk, j):
                        r0 = 0 if j < 0 else max(0, j - 4 * i)
                        specials = {}
                        if is_meta:
                            specials[0] = 3
                        elif j < 0:
                            if i == 0 and diff:
                                specials[0] = 2
                        else:
                            rd = j - 4 * i
                            if 0 <= rd < nsub:
                                specials[rd] = 0
                            if diff and 0 <= rd + 1 < nsub:
                                specials[rd + 1] = 1
                        c_lo = r0 * 128
                        Ps = []
                        for s_ in range(nsm):
                            ps = psS.next()
                            P = Pt[s_].next()

                            def score(c0, c1, first):
                                if diff:
                                    S.op("pe", lambda e: e.matmul(ps[0:nk, c0:c1], K[s_ * 64:(s_ + 1) * 64, kc0:kc0 + nk],
                                                                  Q[s_ * 64:(s_ + 1) * 64, c0:c1], start=first, stop=True),
                                         reads=[K, Q], writes=[ps], inc=True)
                                else:
                                    S.op("pe", lambda e: e.matmul(ps[0:nk, c0:c1], K[:, kc0:kc0 + nk], Q[:, c0:c1],
                                                                  start=first, stop=False),
                                         reads=[K, Q], writes=[ps], inc=False)
                                    S.op("pe", lambda e: e.matmul(ps[0:nk, c0:c1], Krt[0:64, kc0:kc0 + nk], Qrs[0:64, c0:c1],
                                                                  start=False, stop=True),
                                         reads=[Krt, Qrs], writes=[ps], inc=True)
                            done_cols = sorted(specials)
                            for r in done_cols:
                                c0 = r * 128
                                c1 = c0 + qsz
                                bt = bias_sb[0:nk, specials[r] * 128:specials[r] * 128 + qsz] if diff \
                                    else gc[0:nk, GC_MASK:GC_MASK + qsz]
                                S.op("pe", lambda e: e.matmul(ps[0:nk, c0:c1], identf(nk, nk), bt, start=True, stop=False),
                                     reads=[gc, bias_sb], writes=[ps], inc=False)
                                score(c0, c1, False)
                            segs = []
                            cur = c_lo
                            for r in done_cols:
                                if r * 128 > cur:
                                    segs.append((cur, r * 128))
                                cur = r * 128 + qsz
                            if cur < nq:
                                segs.append((cur, nq))
                            for (c0, c1) in segs:
                                score(c0, c1, True)
                            for r in done_cols:
                                c0 = r * 128
                                S.op("act", lambda e: e.activation(out=P[0:nk, c0:c0 + qsz], in_=ps[0:nk, c0:c0 + qsz],
                                                                   func=AF.Exp), reads=[ps], writes=[P])
                            for (c0, c1) in segs:
                                if diff:
                                    S.op("act", lambda e: e.activation(out=P[0:nk, c0:c1], in_=ps[0:nk, c0:c1], func=AF.Exp,
                                                                       bias=gc[0:nk, GC_B15 + h:GC_B15 + h + 1]),
                                         reads=[ps, gc], writes=[P])
                                else:
                                    S.op("act", lambda e: e.activation(out=P[0:nk, c0:c1], in_=ps[0:nk, c0:c1], func=AF.Exp),
                                         reads=[ps], writes=[P])
                            Ps.append(P)
                        return Ps

                    def do_pv(kc0, nk, vblk, j, Ps):
                        r0 = 0 if j < 0 else max(0, j - 4 * i)
                        c_lo = r0 * 128
                        last = is_meta or (j == 4 * i + nsub - 1)
                        for s_ in range(nsm):
                            P = Ps[s_]
                            ob, lb = Ob[s_]
                            S.op("pe", lambda e: e.matmul(ob[:, c_lo:nq], V[0:nk, vblk, :], P[0:nk, c_lo:nq],
                                                          start=(j < 0), stop=last, skip_group_check=True),
                                 reads=[P, V], writes=[ob], inc=False)
                            S.op("pe", lambda e: e.matmul(lb[:, c_lo:nq], ones_bf[0:nk, :], P[0:nk, c_lo:nq],
                                                          start=(j < 0), stop=last, skip_group_check=True),
                                 reads=[P, ones_bf], writes=[lb], inc=True)

                    pend = None
                    for kb in kbs:
                        cur_ = do_scores(*kb)
                        if pend is not None:
                            do_pv(*pend)
                        pend = kb + (cur_,)
                    do_pv(*pend)

                    def recip(lb):
                        rr = rrot.next()
                        S.op("act", lambda e: e.activation(out=rr[:, 0:nq], in_=lb[:, 0:nq], func=AF.Ln), reads=[lb], writes=[rr])
                        S.op("act", lambda e: e.activation(out=rr[:, 0:nq], in_=rr[:, 0:nq], func=AF.Exp, scale=-1.0),
                             reads=[rr], writes=[rr])
                        return rr
                    oT = oTrot.next()
                    if diff:
                        (o1, l1), (o2, l2) = Ob
                        r1 = recip(l1)
                        r2 = recip(l2)
                        t1 = trot.next()
                        S.op("dve", lambda e: e.tensor_tensor(out=t1[:, 0:nq], in0=o1[:, 0:nq], in1=r1[:, 0:nq], op=ALU.mult),
                             reads=[o1, r1], writes=[t1])
                        t2 = trot.next()
                        S.op("dve", lambda e: e.tensor_tensor(out=t2[:, 0:nq], in0=o2[:, 0:nq], in1=r2[:, 0:nq], op=ALU.mult),
                             reads=[o2, r2], writes=[t2])
                        S.op("dve", lambda e: e.scalar_tensor_tensor(out=t1[:, 0:nq], in0=t2[:, 0:nq], scalar=lamv[:, 5:6],
                                                                     in1=t1[:, 0:nq], op0=ALU.mult, op1=ALU.add),
                             reads=[t2, t1, lamv], writes=[t1])
                        S.op("pool", lambda e: e.tensor_tensor(out=t2[:, 0:nq], in0=t1[:, 0:nq], in1=t1[:, 0:nq], op=ALU.mult),
                             reads=[t1], writes=[t2])
                        pss = psS.next()
                        S.op("pe", lambda e: e.matmul(pss[:, 0:nq], ones_f[:, :], t2[:, 0:nq], start=True, stop=True),
                             reads=[t2, ones_f], writes=[pss], inc=True)
                        S.op("dve", lambda e: e.tensor_scalar(out=t2[:, 0:nq], in0=pss[:, 0:nq], scalar1=1.0 / 128,
                                                              scalar2=NORM_EPS, op0=ALU.mult, op1=ALU.add),
                             reads=[pss], writes=[t2])
                        S.op("act", lambda e: e.activation(out=t2[:, 0:nq], in_=t2[:, 0:nq], func=AF.Ln), reads=[t2], writes=[t2])
                        S.op("act", lambda e: e.activation(out=t2[:, 0:nq], in_=t2[:, 0:nq], func=AF.Exp, scale=-0.5),
                             reads=[t2], writes=[t2])
                        S.op("dve", lambda e: e.scalar_tensor_tensor(out=oT[:, 0:nq], in0=t1[:, 0:nq], scalar=lamv[:, 6:7],
                                                                     in1=t2[:, 0:nq], op0=ALU.mult, op1=ALU.mult),
                             reads=[t1, t2, lamv], writes=[oT])
                    else:
                        (o1, l1), = Ob
                        r1 = recip(l1)
                        S.op("dve", lambda e: e.tensor_tensor(out=oT[:, 0:nq], in0=o1[:, 0:nq], in1=r1[:, 0:nq], op=ALU.mult),
                             reads=[o1, r1], writes=[oT])
                    S.dma("sp", O_dst, O_dst[h * 128:(h + 1) * 128, qp0:qp0 + nq], oT, oT[:, 0:nq])
            S.end_scope()

        attention("diff")
        if stop == "Bd":
            S.barrier()
            return nc, S
        attention("mla")
        if stop == "Bm":
            S.barrier()
            return nc, S

        S.begin_scope()
        h_sb = S.sbuf("h", [128, 16, n], F32)
        XH = S.sbuf("XH", [128, 16, n], BF16)
        gtrot = Rot(S.sbuf("gt", [128, n], F32, n=4))
        mixed = S.sbuf("mixed", [128, 16, n], F32)
        act = S.sbuf("act", [128, 44, n], BF16)
        tfrot = Rot(S.sbuf("tf", [128, n], F32, n=3))
        sqrot = Rot(S.sbuf("sq", [128, n], BF16, n=3))
        rstd = S.sbuf("rstd", [128, n], F32)
        psr = Rot(PS[0:3])
        PSD = PS[3:7]
        ps_s1 = PS[7]
        jobs = []
        for t in range(NT):
            p0 = t * n

            def j_load(ws, wv, p0=p0):
                S.dma("sp", h_sb, h_sb[:, :, :], h_in, h_in[:, p0:p0 + n].rearrange("(c p) t -> p c t", p=128))
            jobs.append((None, j_load))
            for bi, (XT, wd) in enumerate(((csT, w_brc_d), (oTd, w_brd_d), (oTm, w_brm_d))):
                def j_x(ws, wv, XT=XT, p0=p0):
                    S.dma("sp", XH, XH[:, 0:8, :], XT, XT[:, p0:p0 + n].rearrange("(c p) t -> p c t", p=128))
                jobs.append((None, j_x))
                for g in range(4):
                    def j_br(ws, wv, g=g, bi=bi, p0=p0):
                        gts = []
                        for m in range(4):
                            gm = g * 4 + m
                            gt = gtrot.next()
                            S.dma("sp", gt, gt[:, :], G, G[bi * D_MODEL + gm * 128:bi * D_MODEL + (gm + 1) * 128, p0:p0 + n])
                            gts.append(gt)
                        for m in range(4):
                            gm = g * 4 + m
                            gt = gts[m]
                            ps = psr.next()
                            fm_mm(wv, ws, 8, m, 128, XH, lambda c: XH[:, c, :], ps, n)
                            if bi == 0:
                                S.op("dve", lambda e: e.tensor_tensor(out=mixed[:, gm, :], in0=ps[:, 0:n], in1=gt[:, :], op=ALU.mult),
                                     reads=[ps, gt], writes=[mixed])
                            else:
                                tf = tfrot.next()
                                S.op("dve", lambda e: e.tensor_tensor(out=tf[:, :], in0=ps[:, 0:n], in1=gt[:, :], op=ALU.mult),
                                     reads=[ps, gt], writes=[tf])
                                if bi == 1:
                                    S.op("dve", lambda e: e.tensor_tensor(out=mixed[:, gm, :], in0=mixed[:, gm, :], in1=tf[:, :], op=ALU.add),
                                         reads=[mixed, tf], writes=[mixed])
                                else:
                                    S.op("dve", lambda e: e.tensor_tensor(out=act[:, gm, :], in0=mixed[:, gm, :], in1=tf[:, :], op=ALU.add),
                                         reads=[mixed, tf], writes=[act])
                    jobs.append((wspec(wd, l, 0, 8, g * 512, 512), j_br))
            for g in range(4):
                def j_out(ws, wv, g=g):
                    for m in range(4):
                        gm = g * 4 + m
                        ps = psr.next()
                        fm_mm(wv, ws, 16, m, 128, act, lambda c: act[:, c, :], ps, n)
                        S.op("dve", lambda e: e.tensor_tensor(out=h_sb[:, gm, :], in0=h_sb[:, gm, :], in1=ps[:, 0:n], op=ALU.add),
                             reads=[h_sb, ps], writes=[h_sb])
                jobs.append((wspec(w_out_d, l, 0, 16, g * 512, 512), j_out))

            def j_fnorm(ws, wv):
                rmsnorm_fm(h_sb, lambda c: h_sb[:, c, :], 16, lambda c: lc[:, LC_GFFN + c:LC_GFFN + c + 1], D_MODEL,
                           XH, lambda c: XH[:, c, :], sqrot, rstd, ps_s1, n)
            jobs.append((None, j_fnorm))
            xh_rhs = lambda c: XH[:, c, :]
            for fg in range(11):
                def j_gate(ws, wv, fg=fg):
                    for m in range(4):
                        ps = psr.next()
                        fm_mm(wv, ws, 16, m, 128, XH, xh_rhs, ps, n)
                        S.op("act", lambda e: e.activation(out=mixed[:, m, :], in_=ps[:, 0:n], func=AF.Silu),
                             reads=[ps], writes=[mixed])
                jobs.append((wspec(w_fg_d, l, 0, 16, fg * 512, 512), j_gate))

                def j_up(ws, wv, fg=fg):
                    for m in range(4):
                        ps = psr.next()
                        fm_mm(wv, ws, 16, m, 128, XH, xh_rhs, ps, n)
                        S.op("dve", lambda e: e.tensor_tensor(out=act[:, fg * 4 + m, :], in0=ps[:, 0:n], in1=mixed[:, m, :], op=ALU.mult),
                             reads=[ps, mixed], writes=[act])
                jobs.append((wspec(w_fu_d, l, 0, 16, fg * 512, 512), j_up))
            for cg in range(4):
                fgs = [(0, 16), (16, 16), (32, 12)]
                for fi, (f0, kc) in enumerate(fgs):
                    def j_down(ws, wv, cg=cg, fi=fi, f0=f0, kc=kc, p0=p0):
                        for m in range(4):
                            fm_mm(wv, ws, kc, m, 128, act, lambda c: act[:, f0 + c, :], PSD[m], n,
                                  start=(fi == 0), stop=(fi == 2))
                        if fi == 2:
                            for m in range(4):
                                gm = cg * 4 + m
                                S.op("dve", lambda e: e.tensor_tensor(out=h_sb[:, gm, :], in0=h_sb[:, gm, :], in1=PSD[m][:, 0:n], op=ALU.add),
                                     reads=[h_sb, PSD[m]], writes=[h_sb])
                            if cg == 3:
                                S.dma("sp", h_out, h_out[:, p0:p0 + n].rearrange("(c p) t -> p c t", p=128), h_sb, h_sb[:, :, :])
                    jobs.append((wspec(w_fd_d, l, f0 * 128, kc, cg * 512, 512), j_down))
        run_jobs(jobs)
        S.end_scope()

    S.begin_scope()
    h_fin = hT[DEPTH % 2]
    h_sb = S.sbuf("h", [128, 16, n], F32)
    hnf = S.sbuf("hnf", [128, 16, n], F32)
    sqrot = Rot(S.sbuf("sq", [128, n], BF16, n=3))
    rstd = S.sbuf("rstd", [128, n], F32)
    strot = Rot(S.sbuf("ostg", [128, D_MODEL], F32, n=2))
    psr = Rot(PS[0:6])
    for t in range(NT):
        p0 = t * n
        S.dma("sp", h_sb, h_sb[:, :, :], h_fin, h_fin[:, p0:p0 + n].rearrange("(c p) t -> p c t", p=128))
        rmsnorm_fm(h_sb, lambda c: h_sb[:, c, :], 16, lambda c: gc[:, GC_FIN + c:GC_FIN + c + 1], D_MODEL,
                   hnf, lambda c: hnf[:, c, :], sqrot, rstd, PS[7], n)
        for (s0, sz) in subblocks(n):
            st = strot.next()
            for g4 in range(4):
                ps = psr.next()
                for q in range(4):
                    c = g4 * 4 + q
                    S.op("pe", lambda e, c=c, q=q: e.transpose(ps[0:sz, q * 128:(q + 1) * 128], hnf[:, c, s0:s0 + sz],
                                                               identf(128, 128)),
                         reads=[hnf, gc], writes=[ps], inc=(q == 3))
                if g4 % 2 == 0:
                    S.op("dve", lambda e: e.tensor_copy(st[0:sz, g4 * 512:(g4 + 1) * 512], ps[0:sz, :]), reads=[ps], writes=[st])
                else:
                    S.op("act", lambda e: e.activation(out=st[0:sz, g4 * 512:(g4 + 1) * 512], in_=ps[0:sz, :], func=AF.Copy),
                         reads=[ps], writes=[st])
            pa = p0 + s0
            lo = max(pa, N_META)
            if lo < pa + sz:
                S.dma("sp", out_d, out_d[lo - N_META:pa + sz - N_META, :], st, st[lo - pa:sz, :])
    S.end_scope()
    S.barrier()
    return nc, S


def _t5_bucket(rel):
    nb = 16
    ret = np.where(rel > 0, nb, 0)
    nn = np.abs(rel)
    max_exact = 8
    nf = np.maximum(nn, 1).astype(np.float32)
    large = max_exact + (np.log(nf / max_exact) / math.log(128 / max_exact) * (nb - max_exact)).astype(np.int32)
    large = np.minimum(large, nb - 1)
    return ret + np.where(nn < max_exact, nn, large)


def _chunk_id(pos):
    return np.where(pos < N_META, 0, (pos - N_META) // 64 + 1)


def host_consts(inp, SEQ, DEPTH):
    L = SEQ + N_META
    f32 = np.float32
    perm = np.concatenate([np.arange(32, 64), np.arange(0, 32)])
    w_in = np.ascontiguousarray(inp["w_in"], dtype=f32)
    w_crs = np.ascontiguousarray(w_in[:, :, OFF_CR + perm])
    w_uq = inp["w_uq"].reshape(DEPTH, 512, 8, 192)
    w_uqn = np.ascontiguousarray(w_uq[:, :, :, 0:128].reshape(DEPTH, 512, 1024))
    rope_n = w_uq[:, :, :, 128:192]
    rope_s = rope_n[:, :, :, perm]
    w_uqr = np.ascontiguousarray(np.concatenate([rope_n.reshape(DEPTH, 512, 512), rope_s.reshape(DEPTH, 512, 512)], axis=2))
    w_ukv = inp["w_ukv"].reshape(DEPTH, 256, 8, 256)
    w_kn = np.ascontiguousarray(w_ukv[:, :, :, 0:128].reshape(DEPTH, 256, 1024))
    w_kv = np.ascontiguousarray(w_ukv[:, :, :, 128:256].reshape(DEPTH, 256, 1024))
    lc = np.zeros((DEPTH, 128, NLC), f32)
    fm = lambda v, nch: v.reshape(nch, 128).T
    for l in range(DEPTH):
        lc[l, :, LC_GMIX:LC_GMIX + 16] = fm(inp["norm_mix"][l], 16)
        lc[l, :, LC_GFFN:LC_GFFN + 16] = fm(inp["norm_ffn"][l], 16)
        cw = inp["conv_w"][l]
        lc[l, :, LC_CW:LC_CW + 248] = cw.T.reshape(8, 128, 31).transpose(1, 0, 2).reshape(128, 248)
        lc[l, :, LC_CB:LC_CB + 8] = fm(inp["conv_b"][l], 8)
        lc[l, :, LC_LNG:LC_LNG + 8] = fm(inp["conv_ln_g"][l], 8)
        lc[l, :, LC_LNB:LC_LNB + 8] = fm(inp["conv_ln_b"][l], 8)
        lc[l, :, LC_QNG:LC_QNG + 4] = fm(inp["mla_q_norm"][l], 4)
        lc[l, :, LC_KVG:LC_KVG + 2] = fm(inp["mla_kv_norm"][l], 2)
        lc[l, :, LC_SUB:LC_SUB + 128] = inp["diff_subln"][l][None, :]
        lc[l, :, LC_SUBC] = inp["diff_subln"][l]
        lc[l, :, LC_LQ1:LC_LQ1 + 64] = inp["diff_lq1"][l][None, :]
        lc[l, :, LC_LK1:LC_LK1 + 64] = inp["diff_lk1"][l][None, :]
        lc[l, :, LC_LQ2:LC_LQ2 + 64] = inp["diff_lq2"][l][None, :]
        lc[l, :, LC_LK2:LC_LK2 + 64] = inp["diff_lk2"][l][None, :]
    rel = np.asarray(inp["rel_bias"], f32)
    gcm = np.zeros((128, NGC), f32)
    gcm[:, GC_FIN:GC_FIN + 16] = fm(np.asarray(inp["final_norm"], f32), 16)
    gcm[:, GC_B15:GC_B15 + 8] = rel[15][None, :]
    gcm[:, GC_ID:GC_ID + 128] = np.eye(128, dtype=f32)
    def tile(kpos, qpos):
        relm = kpos[:, None] - qpos[None, :]
        vis = _chunk_id(kpos)[:, None] <= _chunk_id(qpos)[None, :]
        b = rel[_t5_bucket(relm)]
        return np.where(vis[:, :, None], b, f32(NEG)), vis
    ar = np.arange(128)
    biasT = np.zeros((128, 8, 4, 128), f32)
    d, vis_d = tile(N_META + 128 + ar, N_META + 128 + ar)
    biasT[:, :, 0, :] = d.transpose(0, 2, 1)
    p, _ = tile(N_META + ar, N_META + 128 + ar)
    biasT[:, :, 1, :] = p.transpose(0, 2, 1)
    m0, _ = tile(np.arange(N_META), N_META + ar)
    biasT[0:N_META, :, 2, :] = m0.transpose(0, 2, 1)
    mm, _ = tile(np.arange(N_META), np.arange(N_META))
    biasT[0:N_META, :, 3, 0:N_META] = mm.transpose(0, 2, 1)
    gcm[:, GC_MASK:GC_MASK + 128] = np.where(vis_d, f32(0.0), f32(NEG))
    half = 32
    inv = (10000.0 ** (-np.arange(half, dtype=f32) / half)).astype(f32)
    ang = np.arange(L, dtype=f32)[None, :] * inv[:, None]
    cos, sin = np.cos(ang).astype(f32), np.sin(ang).astype(f32)
    C64 = np.concatenate([cos, cos], 0)
    S64 = np.concatenate([-sin, sin], 0)
    ropeC = np.ascontiguousarray(np.concatenate([C64, C64], 0))
    ropeS = np.ascontiguousarray(np.concatenate([S64, S64], 0))
    shared = dict(
        meta=np.ascontiguousarray(inp["meta_tokens"], dtype=f32), w_in=w_in, w_crs=w_crs, w_uqn=w_uqn, w_uqr=w_uqr,
        w_kn=w_kn, w_kv=w_kv, w_br_conv=np.ascontiguousarray(inp["w_br_conv"], dtype=f32),
        w_br_diff=np.ascontiguousarray(inp["w_br_diff"], dtype=f32), w_br_mla=np.ascontiguousarray(inp["w_br_mla"], dtype=f32),
        w_out=np.ascontiguousarray(inp["w_out"], dtype=f32), w_ffn_gate=np.ascontiguousarray(inp["w_ffn_gate"], dtype=f32),
        w_ffn_up=np.ascontiguousarray(inp["w_ffn_up"], dtype=f32), w_ffn_down=np.ascontiguousarray(inp["w_ffn_down"], dtype=f32),
        lc=lc, gc=gcm, biasT=np.ascontiguousarray(biasT.reshape(128, 8 * 4 * 128)), ropeC=ropeC, ropeS=ropeS)
    return shared


def run(inp, TT=456, debug=False, trace=False, stop=None, ncores=4):
    inp = {k: np.asarray(v) for k, v in inp.items()}
    x = inp["x"]
    B, SEQ, _ = x.shape
    DEPTH = inp["w_in"].shape[0]
    nc, S = build(SEQ, DEPTH, TT, debug=debug, stop=stop)
    print('built: ninstr', S.ninstr, 'nsems', len(S.owners) + 5, flush=True)
    shared = host_consts(inp, SEQ, DEPTH)
    in_maps = []
    for c in range(ncores):
        m = dict(shared)
        m["x"] = np.ascontiguousarray(x[c % B], dtype=np.float32)
        in_maps.append(m)
    res = run_bass_kernel_spmd(nc, in_maps, core_ids=list(range(ncores)), **({"trace": True} if trace else {}))
    out = np.stack([res.results[b % ncores]["out"] for b in range(B)], axis=0).astype(np.float32)
    return out, res


def kernel(**inputs):
    out, _ = run(inputs)
    return out
```
